# Optimizing a Trainium2 kernel written in Bass

```python
import math
import jax, jax.numpy as jnp
from jax import lax
import numpy as np


D_MODEL = 1024
BATCH = 16
SEQ = 4096
DEPTH = 1

ATTN_HEADS = 8
QK_NOPE_DIM = 64
QK_ROPE_DIM = 32
V_HEAD_DIM = 64
Q_LORA_RANK = 256
KV_LORA_RANK = 128
ATTN_WIDTH = ATTN_HEADS * V_HEAD_DIM
ROPE_THETA = 10000.0
Q_BLOCK = 128

HY_WIDTH = D_MODEL - ATTN_WIDTH
HY_ORDER = 2
HY_GROUPS = 8
HY_SHORT = 3
HY_EMB_DIM = 33
HY_FILTER_HIDDEN = 64
HY_FAST_DECAY = 0.3
HY_SLOW_DECAY = 1.5
HY_TARGET = 1e-2

OFF_CQ = Q_LORA_RANK
OFF_CKV = OFF_CQ + KV_LORA_RANK
OFF_KR = OFF_CKV + QK_ROPE_DIM
IN_WIDTH = OFF_KR + (HY_ORDER + 1) * HY_WIDTH

PEER_HEADS = 8
PEER_NKEYS = 128
PEER_N_EXPERTS = PEER_NKEYS * PEER_NKEYS
PEER_DK = 128
PEER_TOPK = 16
PEER_CHUNK = 128

ALPHA = (2 * DEPTH) ** 0.25
BETA = (8 * DEPTH) ** -0.25
LN_EPS = 1e-5
RMS_EPS = 1e-6

kernel_name = "hymba_mla_hyena_peer_deepnorm_encoder"


def layer_norm(x, g, b):
    xf = x.astype(jnp.float32)
    mu = jnp.mean(xf, axis=-1, keepdims=True)
    var = jnp.mean(jnp.square(xf - mu), axis=-1, keepdims=True)
    return ((xf - mu) * lax.rsqrt(var + LN_EPS) * g.astype(jnp.float32) + b.astype(jnp.float32)).astype(x.dtype)


def rms_norm(x, g):
    xf = x.astype(jnp.float32)
    y = xf * lax.rsqrt(jnp.mean(jnp.square(xf), axis=-1, keepdims=True) + RMS_EPS)
    return (y * g.astype(jnp.float32)).astype(x.dtype)


def rotary_tables(L, dim):
    inv = 1.0 / (ROPE_THETA ** (jnp.arange(0, dim, 2, dtype=jnp.float32) / dim))
    ang = jnp.arange(L, dtype=jnp.float32)[:, None] * inv[None, :]
    return jnp.cos(ang), jnp.sin(ang)


def apply_rope(x, cos, sin):
    half = x.shape[-1] // 2
    xf = x.astype(jnp.float32)
    x1, x2 = xf[..., :half], xf[..., half:]
    return jnp.concatenate([x1 * cos - x2 * sin, x1 * sin + x2 * cos], axis=-1).astype(x.dtype)


def mla_attention(c_q, c_kv, k_rope, q_norm_g, w_uq, kv_norm_g, w_ukv):
    B, S, _ = c_q.shape
    q = (rms_norm(c_q, q_norm_g) @ w_uq).reshape(B, S, ATTN_HEADS, QK_NOPE_DIM + QK_ROPE_DIM)
    kv = (rms_norm(c_kv, kv_norm_g) @ w_ukv).reshape(B, S, ATTN_HEADS, QK_NOPE_DIM + V_HEAD_DIM)
    q_nope, q_pe = q[..., :QK_NOPE_DIM], q[..., QK_NOPE_DIM:]
    k_nope, v = kv[..., :QK_NOPE_DIM], kv[..., QK_NOPE_DIM:]
    cos, sin = rotary_tables(S, QK_ROPE_DIM)
    q_pe = apply_rope(q_pe, cos[:, None, :], sin[:, None, :])
    k_pe = apply_rope(k_rope, cos, sin)
    scale = (QK_NOPE_DIM + QK_ROPE_DIM) ** -0.5
    nb = S // Q_BLOCK
    qn_blocks = q_nope.reshape(B, nb, Q_BLOCK, ATTN_HEADS, QK_NOPE_DIM).transpose(1, 0, 2, 3, 4)
    qr_blocks = q_pe.reshape(B, nb, Q_BLOCK, ATTN_HEADS, QK_ROPE_DIM).transpose(1, 0, 2, 3, 4)

    def one_block(qs):
        qn, qr = qs
        s = (jnp.einsum('bqhd,bkhd->bhqk', qn, k_nope, preferred_element_type=jnp.float32)
             + jnp.einsum('bqhr,bkr->bhqk', qr, k_pe, preferred_element_type=jnp.float32)) * scale
        p = jax.nn.softmax(s, axis=-1)
        return jnp.einsum('bhqk,bkhd->bqhd', p.astype(v.dtype), v)

    o = lax.map(one_block, (qn_blocks, qr_blocks))
    return o.transpose(1, 0, 2, 3, 4).reshape(B, S, ATTN_HEADS, V_HEAD_DIM)


def short_conv(u, w, b):
    L = u.shape[1]
    pad = HY_SHORT // 2
    up = jnp.pad(u, ((0, 0), (pad, pad), (0, 0)))
    y = b
    for i in range(HY_SHORT):
        y = y + up[:, i:i + L] * w[i]
    return y


def hyena_position_features(L):
    t = jnp.linspace(0.0, 1.0, L, dtype=jnp.float32)[:, None]
    bands = (HY_EMB_DIM - 1) // 2
    w = 2.0 * math.pi * jnp.arange(L, dtype=jnp.float32) / L
    f = jnp.linspace(1e-4, bands - 1, bands, dtype=jnp.float32)
    ang = w[:, None] * f[None, :]
    return t, jnp.concatenate([t, jnp.cos(ang), -jnp.sin(ang)], axis=-1)


def hyena_filters(L, w1, b1, fr1, w2, b2, fr2, w3):
    f32 = jnp.float32
    t, z = hyena_position_features(L)
    h = jnp.sin(fr1.astype(f32) * (z @ w1.astype(f32) + b1.astype(f32)))
    h = jnp.sin(fr2.astype(f32) * (h @ w2.astype(f32) + b2.astype(f32)))
    h = (h @ w3.astype(f32)).reshape(L, HY_ORDER, 2, HY_WIDTH)
    max_decay = math.log(HY_TARGET) / HY_FAST_DECAY
    min_decay = math.log(HY_TARGET) / HY_SLOW_DECAY
    deltas = jnp.linspace(min_decay, max_decay, HY_WIDTH, dtype=f32)
    decay = jnp.exp(-t * jnp.abs(deltas)[None, :])
    h = h * decay[:, None, None, :]
    return h / (jnp.sum(jnp.abs(h), axis=0, keepdims=True) + 1e-6)


def hyena_mixer(u, short_w, short_b, w1, b1, fr1, w2, b2, fr2, w3, hy_bias):
    B, L, _ = u.shape
    n = 2 * L
    u = short_conv(u, short_w, short_b)
    v = u[..., :HY_WIDTH]
    gates = (u[..., HY_WIDTH:2 * HY_WIDTH], u[..., 2 * HY_WIDTH:])
    h = hyena_filters(L, w1, b1, fr1, w2, b2, fr2, w3)
    hf = jnp.fft.rfft(h, n=n, axis=0)
    h_bidir = hf[:, :, 0] + jnp.conj(hf[:, :, 1])
    z = v
    for o in range(HY_ORDER):
        zf = jnp.fft.rfft(z.astype(jnp.float32), n=n, axis=1)
        y = jnp.fft.irfft(zf * h_bidir[None, :, o], n=n, axis=1)[:, :L]
        z = gates[o] * (y.astype(z.dtype) + hy_bias[o] * z)
    return z


def peer_ffn(x, w_q, sub_keys, u_tab, v_tab):
    B, S, D = x.shape
    T = B * S
    xt = x.reshape(T // PEER_CHUNK, PEER_CHUNK, D)

    def chunk(xc):
        C = xc.shape[0]
        q = (xc @ w_q).reshape(C, PEER_HEADS, 2, PEER_DK // 2)
        s = jnp.einsum('chpd,hpnd->chpn', q, sub_keys, preferred_element_type=jnp.float32)
        sv, si = lax.top_k(s, PEER_TOPK)
        cand = sv[:, :, 0, :, None] + sv[:, :, 1, None, :]
        cidx = si[:, :, 0, :, None] * PEER_NKEYS + si[:, :, 1, None, :]
        cand = cand.reshape(C, PEER_HEADS, PEER_TOPK * PEER_TOPK)
        cidx = cidx.reshape(C, PEER_HEADS, PEER_TOPK * PEER_TOPK)
        best, pos = lax.top_k(cand, PEER_TOPK)
        eidx = jnp.take_along_axis(cidx, pos, axis=-1)
        g = jax.nn.softmax(best, axis=-1)
        u = u_tab[eidx]
        a = jax.nn.gelu(jnp.einsum('chkd,cd->chk', u, xc, preferred_element_type=jnp.float32), approximate=False)
        vv = v_tab[eidx]
        return jnp.einsum('chk,chkd->cd', (g * a).astype(vv.dtype), vv)

    return lax.map(chunk, xt).reshape(B, S, D)


def setup_inputs(seed: int = 0) -> dict:
    key = jax.random.key(seed)
    ks = jax.random.split(key, 32)
    f32 = jnp.float32

    def nrm(k, shape, scale):
        return jax.random.normal(k, shape, f32) * scale

    def gain(k, shape):
        return 1.0 + 0.02 * jax.random.normal(k, shape, f32)

    L_ = DEPTH
    return {
        "x": jax.random.normal(ks[0], (BATCH, SEQ, D_MODEL), f32),
        "emb_ln_g": gain(ks[1], (D_MODEL,)),
        "emb_ln_b": nrm(ks[2], (D_MODEL,), 0.02),
        "w_in": nrm(ks[3], (L_, D_MODEL, IN_WIDTH), D_MODEL ** -0.5),
        "q_norm_g": gain(ks[4], (L_, Q_LORA_RANK)),
        "w_uq": nrm(ks[5], (L_, Q_LORA_RANK, ATTN_HEADS * (QK_NOPE_DIM + QK_ROPE_DIM)), Q_LORA_RANK ** -0.5),
        "kv_norm_g": gain(ks[6], (L_, KV_LORA_RANK)),
        "w_ukv": nrm(ks[7], (L_, KV_LORA_RANK, ATTN_HEADS * (QK_NOPE_DIM + V_HEAD_DIM)), KV_LORA_RANK ** -0.5),
        "hy_short_w": nrm(ks[8], (L_, HY_SHORT, (HY_ORDER + 1) * HY_WIDTH), HY_SHORT ** -0.5),
        "hy_short_b": nrm(ks[9], (L_, (HY_ORDER + 1) * HY_WIDTH), 0.02),
        "hy_filt_w1": nrm(ks[10], (L_, HY_EMB_DIM, HY_FILTER_HIDDEN), HY_EMB_DIM ** -0.5),
        "hy_filt_b1": nrm(ks[11], (L_, HY_FILTER_HIDDEN), 0.02),
        "hy_filt_freq1": gain(ks[12], (L_, HY_FILTER_HIDDEN)),
        "hy_filt_w2": nrm(ks[13], (L_, HY_FILTER_HIDDEN, HY_FILTER_HIDDEN), HY_FILTER_HIDDEN ** -0.5),
        "hy_filt_b2": nrm(ks[14], (L_, HY_FILTER_HIDDEN), 0.02),
        "hy_filt_freq2": gain(ks[15], (L_, HY_FILTER_HIDDEN)),
        "hy_filt_w3": nrm(ks[16], (L_, HY_FILTER_HIDDEN, HY_ORDER * 2 * HY_WIDTH), HY_FILTER_HIDDEN ** -0.5),
        "hy_bias": nrm(ks[17], (L_, HY_ORDER, HY_WIDTH), 0.5),
        "attn_out_g": gain(ks[18], (L_, ATTN_WIDTH)),
        "hy_out_g": gain(ks[19], (L_, HY_WIDTH)),
        "w_o": nrm(ks[20], (L_, ATTN_WIDTH + HY_WIDTH, D_MODEL), BETA * (ATTN_WIDTH + HY_WIDTH) ** -0.5),
        "ln_mix_g": gain(ks[21], (L_, D_MODEL)),
        "ln_mix_b": nrm(ks[22], (L_, D_MODEL), 0.02),
        "peer_wq": nrm(ks[23], (L_, D_MODEL, PEER_HEADS * PEER_DK), D_MODEL ** -0.5),
        "peer_sub_keys": nrm(ks[24], (L_, PEER_HEADS, 2, PEER_NKEYS, PEER_DK // 2), (PEER_DK // 2) ** -0.5),
        "peer_u": nrm(ks[25], (L_, PEER_N_EXPERTS, D_MODEL), D_MODEL ** -0.5),
        "peer_v": nrm(ks[26], (L_, PEER_N_EXPERTS, D_MODEL), BETA * PEER_HEADS ** -0.5),
        "ln_ffn_g": gain(ks[27], (L_, D_MODEL)),
        "ln_ffn_b": nrm(ks[28], (L_, D_MODEL), 0.02),
    }


def reference(x, emb_ln_g, emb_ln_b, w_in, q_norm_g, w_uq, kv_norm_g, w_ukv,
              hy_short_w, hy_short_b, hy_filt_w1, hy_filt_b1, hy_filt_freq1,
              hy_filt_w2, hy_filt_b2, hy_filt_freq2, hy_filt_w3, hy_bias,
              attn_out_g, hy_out_g, w_o, ln_mix_g, ln_mix_b,
              peer_wq, peer_sub_keys, peer_u, peer_v, ln_ffn_g, ln_ffn_b):
    B, S, _ = x.shape
    h = layer_norm(x, emb_ln_g, emb_ln_b)
    for l in range(DEPTH):
        proj = h @ w_in[l]
        c_q = proj[..., :OFF_CQ]
        c_kv = proj[..., OFF_CQ:OFF_CKV]
        k_rope = proj[..., OFF_CKV:OFF_KR]
        hy_in = proj[..., OFF_KR:]
        a = mla_attention(c_q, c_kv, k_rope, q_norm_g[l], w_uq[l], kv_norm_g[l], w_ukv[l])
        a = rms_norm(a, attn_out_g[l].reshape(ATTN_HEADS, V_HEAD_DIM)).reshape(B, S, ATTN_WIDTH)
        y = hyena_mixer(hy_in, hy_short_w[l], hy_short_b[l], hy_filt_w1[l], hy_filt_b1[l], hy_filt_freq1[l],
                        hy_filt_w2[l], hy_filt_b2[l], hy_filt_freq2[l], hy_filt_w3[l], hy_bias[l])
        gw = HY_WIDTH // HY_GROUPS
        y = rms_norm(y.reshape(B, S, HY_GROUPS, gw), hy_out_g[l].reshape(HY_GROUPS, gw)).reshape(B, S, HY_WIDTH)
        mix = jnp.concatenate([a, y], axis=-1) @ w_o[l]
        h = layer_norm(ALPHA * h + mix, ln_mix_g[l], ln_mix_b[l])
        f = peer_ffn(h, peer_wq[l], peer_sub_keys[l], peer_u[l], peer_v[l])
        h = layer_norm(ALPHA * h + f, ln_ffn_g[l], ln_ffn_b[l])
    return h
```

```python
import math
import numpy as np
import ml_dtypes
import concourse.bass as bass
import concourse.mybir as mybir
from concourse.bass_utils import run_bass_kernel_spmd
from contextlib import ExitStack

F32 = mybir.dt.float32
BF16 = mybir.dt.bfloat16
I32 = mybir.dt.int32
ALU = mybir.AluOpType
AF = mybir.ActivationFunctionType
AX = mybir.AxisListType

NDS = 48
NSW = 8
D = 1024
NEG = -1.0e30


class Buf:
    __slots__ = ("w", "r")

    def __init__(self):
        self.w = None
        self.r = {}


class KB:
    def __init__(self, nc, es):
        self.nc = nc
        self.eng = {"pe": nc.tensor, "dve": nc.vector, "act": nc.scalar,
                    "pool": nc.gpsimd, "sp": nc.sync}
        self.sem = {n: es.enter_context(nc.semaphore("s_" + n)) for n in self.eng}
        self.cnt = {n: 0 for n in self.eng}
        self.seen = {n: {} for n in self.eng}
        self.dsem = [es.enter_context(nc.semaphore("d%d" % i)) for i in range(NDS)]
        self.dcnt = [0] * NDS
        self.dnext = 0
        self.dnext_sw = 0

    def _wait(self, e, tok):
        if tok is None:
            return
        key, val = tok
        if self.seen[e].get(key, 0) >= val:
            return
        self.seen[e][key] = val
        sem = self.sem[key] if isinstance(key, str) else self.dsem[key]
        self.eng[e].wait_ge(sem, val)

    def _deps(self, e, reads, writes, pe_acc=False):
        for b in reads:
            self._wait(e, b.w)
        for b in writes:
            if not (pe_acc and b.w is not None and b.w[0] == "pe"):
                self._wait(e, b.w)
            for k, v in b.r.items():
                self._wait(e, (k, v))

    def op(self, e, fn, reads=(), writes=(), pe_acc=False):
        self._deps(e, reads, writes, pe_acc)
        ins = fn(self.eng[e])
        self.cnt[e] += 1
        c = self.cnt[e]
        ins.then_inc(self.sem[e], 1)
        for b in reads:
            if b.r.get(e, 0) < c:
                b.r[e] = c
        for b in writes:
            b.w = (e, c)
            b.r = {}

    def dma(self, q, fn, reads=(), writes=()):
        if q == "pool":
            i = self.dnext_sw
            self.dnext_sw = (i + 1) % NSW
        else:
            i = NSW + self.dnext
            self.dnext = (self.dnext + 1) % (NDS - NSW)
        if self.dcnt[i] > 0:
            self._wait(q, (i, self.dcnt[i]))
        self._deps(q, reads, writes)
        ins = fn(self.eng[q])
        self.dcnt[i] += 16
        v = self.dcnt[i]
        ins.then_inc(self.dsem[i], 16)
        for b in reads:
            if b.r.get(i, 0) < v:
                b.r[i] = v
        for b in writes:
            b.w = (i, v)
            b.r = {}

    def barrier(self):
        for e in self.eng:
            for o in self.eng:
                if o != e and self.cnt[o] > 0:
                    self._wait(e, (o, self.cnt[o]))
            for i in range(NDS):
                if self.dcnt[i] > 0:
                    self._wait(e, (i, self.dcnt[i]))

    def drain(self):
        for i in range(NDS):
            if self.dcnt[i] > 0:
                self._wait("sp", (i, self.dcnt[i]))


class T:
    def __init__(self, nc, es, name, shape, dt, psum=False, nb=1):
        f = nc.psum_tensor if psum else nc.sbuf_tensor
        self.t = es.enter_context(f(name, list(shape), dt))
        self.bs = [Buf() for _ in range(nb)]
        self.b = self.bs[0]

    def __getitem__(self, k):
        return self.t[k]


def bcast(ap, shape, axis):
    return ap.unsqueeze(axis).to_broadcast(list(shape))


def build(S, NB):
    NT = S // 128
    NG = S // 512
    N2 = 2 * NT
    n = 2 * S
    nc = bass.Bass("TRN2", target_bir_lowering=False)

    def din(name, shape, dt=F32):
        return nc.dram_tensor(name, list(shape), dt, kind="ExternalInput").ap()

    x = din("x", [NB, S, D])
    emb_g = din("emb_ln_g", [1, D]); emb_b = din("emb_ln_b", [1, D])
    w_in = din("w_in", [D, 1952])
    q_norm_g = din("q_norm_g", [1, 256]); w_uq = din("w_uq", [256, 768])
    kv_norm_g = din("kv_norm_g", [1, 128]); w_ukv = din("w_ukv", [128, 1024])
    hy_short_w = din("hy_short_w", [1, 3 * 1536]); hy_short_b = din("hy_short_b", [1, 1536])
    fw1 = din("hy_filt_w1", [33, 64]); fb1 = din("hy_filt_b1", [64, 1]); ffr1 = din("hy_filt_freq1", [64, 1])
    fw2 = din("hy_filt_w2", [64, 64]); fb2 = din("hy_filt_b2", [64, 1]); ffr2 = din("hy_filt_freq2", [64, 1])
    fw3 = din("hy_filt_w3", [64, 2048])
    hy_bias = din("hy_bias", [1, 1024])
    attn_out_g = din("attn_out_g", [1, 512]); hy_out_g = din("hy_out_g", [1, 512])
    w_o = din("w_o", [D, D])
    ln_mix_g = din("ln_mix_g", [1, D]); ln_mix_b = din("ln_mix_b", [1, D])
    peer_wq = din("peer_wq", [D, D])
    peer_keys = din("peer_sub_keys", [8, 2, 128, 64])
    peer_u = din("peer_u", [16384, D]); peer_v = din("peer_v", [16384, D])
    ln_ffn_g = din("ln_ffn_g", [1, D]); ln_ffn_b = din("ln_ffn_b", [1, D])
    c_rope = din("c_rope", [S, 32])
    c_zT = din("c_zT", [33, S])
    c_tneg = din("c_tneg", [128, NT])
    c_absd = din("c_absd", [1, 512])
    c_fwd = din("c_fwd", [N2, 128, NT, 128], BF16)
    c_inv = din("c_inv", [NT, 128, N2, 128], BF16)
    out = nc.dram_tensor("out", [NB, S, D], F32, kind="ExternalOutput").ap()

    def dscr(name, shape, dt):
        return nc.dram_tensor(name, list(shape), dt).ap()

    Hs = dscr("Hs", [NB, S, D], F32)
    QTs = dscr("QTs", [NB, 8, 96, S], BF16)
    KTs = dscr("KTs", [NB, 8, 96, S], BF16)
    Vs = dscr("Vs", [NB, S, 8 * 65], BF16)
    Us = dscr("Us", [NB, S, 1536], BF16)
    As = dscr("As", [NB, S, 512], BF16)
    Ys = dscr("Ys", [NB, S, 512], BF16)
    HSs = dscr("HSs", [2, 2, S, 512], BF16)
    Hfs = dscr("Hfs", [2, NT, 128, 2, 512], BF16)
    bHs = [[Buf() for _ in range(NT)] for _ in range(NB)]
    bQK = [[Buf() for _ in range(NG)] for _ in range(NB)]
    bVs = [Buf() for _ in range(NB)]
    bUs = [[Buf() for _ in range(NT)] for _ in range(NB)]
    bAs = [Buf() for _ in range(NB)]
    bYs = [[Buf() for _ in range(NT)] for _ in range(NB)]
    bHSs = [Buf() for _ in range(2)]
    bHfs = [[Buf() for _ in range(NT)] for _ in range(2)]
    H2s = dscr("H2s", [NB, S, D], F32)
    H2Ts = dscr("H2Ts", [NB, NT, 128, 8, 128], BF16)
    IJG = dscr("IJG", [NB, NT, 128, 3, 128], F32)
    UTs = dscr("UTs", [128, 128, 8, 128], BF16)
    VBs = dscr("VBs", [128, 128, D], BF16)
    bP5 = [[Buf() for _ in range(NT)] for _ in range(NB)]
    bUT = Buf(); bVB = Buf()

    with ExitStack() as es:
        kb = KB(nc, es)

        def V(fn, r=(), w=()):
            kb.op("dve", fn, [t.b if isinstance(t, T) else t for t in r], [t.b if isinstance(t, T) else t for t in w])

        def A(fn, r=(), w=()):
            kb.op("act", fn, [t.b if isinstance(t, T) else t for t in r], [t.b if isinstance(t, T) else t for t in w])

        def G(fn, r=(), w=()):
            kb.op("pool", fn, [t.b if isinstance(t, T) else t for t in r], [t.b if isinstance(t, T) else t for t in w])

        def PE(fn, r=(), w=(), acc=False):
            kb.op("pe", fn, [t.b if isinstance(t, T) else t for t in r], [t.b if isinstance(t, T) else t for t in w], pe_acc=acc)

        def DMA(q, o, i, r=(), w=()):
            kb.dma(q, lambda e: e.dma_start(out=o, in_=i), [t.b if isinstance(t, T) else t for t in r], [t.b if isinstance(t, T) else t for t in w])

        PS = [T(nc, es, "ps%d" % i, [128, 512], F32, psum=True) for i in range(8)]
        ident = T(nc, es, "ident", [128, 128], F32)
        iot = T(nc, es, "iot", [128, 128], I32)
        G(lambda e: e.iota(iot[:], pattern=[[1, 128]], base=0, channel_multiplier=-1), w=[iot])
        V(lambda e: e.tensor_scalar(ident[:], iot[:], 0, None, op0=ALU.is_equal), r=[iot], w=[ident])
        ones_f = T(nc, es, "ones_f", [128, 128], F32)
        V(lambda e: e.memset(ones_f[:], 1.0), w=[ones_f])
        cst = T(nc, es, "cst", [128, 4], F32)
        V(lambda e: e.memset(cst[:, 0:1], 1e-5), w=[cst])
        V(lambda e: e.memset(cst[:, 1:2], 1e-6), w=[cst])
        V(lambda e: e.memset(cst[:, 2:3], math.pi / 2), w=[cst])

        mhalf = T(nc, es, "mhalf", [128, 32], F32)
        V(lambda e: e.memset(mhalf[:], -0.5), w=[mhalf])

        def load_bc(es_, name, src, width):
            t = T(nc, es_, name, [128, width], F32)
            DMA("sp", t[:], src.partition_broadcast(128), w=[t])
            return t

        def rstd_from_ssq(st, col_in, col_out, inv_n, eps_col):
            V(lambda e: e.tensor_scalar(st[:, col_out], st[:, col_in], inv_n, cst[:, eps_col:eps_col + 1], op0=ALU.mult, op1=ALU.add), r=[st, cst], w=[st])
            G(lambda e: e.tensor_tensor(st[:, col_out], st[:, col_out], mhalf[:, col_out], op=ALU.pow), r=[st, mhalf], w=[st])

        def layer_norm(src, srcdeps, dst, g_bc, b_bc, st, junk):
            A(lambda e: e.activation(junk[:], src, AF.Square, accum_out=st[:, 0:1]), r=srcdeps, w=[junk, st])
            V(lambda e: e.reduce_sum(out=st[:, 1:2], in_=src, axis=AX.X), r=srcdeps, w=[st])
            V(lambda e: e.tensor_scalar(st[:, 1:2], st[:, 1:2], 1.0 / D, None, op0=ALU.mult), r=[st], w=[st])
            V(lambda e: e.tensor_tensor(st[:, 2:3], st[:, 1:2], st[:, 1:2], op=ALU.mult), r=[st], w=[st])
            V(lambda e: e.scalar_tensor_tensor(out=st[:, 3:4], in0=st[:, 0:1], scalar=1.0 / D, in1=st[:, 2:3], op0=ALU.mult, op1=ALU.subtract), r=[st], w=[st])
            V(lambda e: e.tensor_scalar(st[:, 3:4], st[:, 3:4], cst[:, 0:1], None, op0=ALU.add), r=[st, cst], w=[st])
            G(lambda e: e.tensor_tensor(st[:, 3:4], st[:, 3:4], mhalf[:, 3:4], op=ALU.pow), r=[st, mhalf], w=[st])
            V(lambda e: e.scalar_tensor_tensor(out=st[:, 4:5], in0=st[:, 1:2], scalar=-1.0, in1=st[:, 3:4], op0=ALU.mult, op1=ALU.mult), r=[st], w=[st])
            A(lambda e: e.activation(dst[:], src, AF.Identity, bias=st[:, 4:5], scale=st[:, 3:4]), r=list(srcdeps) + [st], w=[dst])
            V(lambda e: e.tensor_tensor(dst[:], dst[:], g_bc[:], op=ALU.mult), r=[dst, g_bc], w=[dst])
            G(lambda e: e.tensor_tensor(dst[:], dst[:], b_bc[:], op=ALU.add), r=[dst, b_bc], w=[dst])

        def layer_norm_gen(src, srcdeps, dst, g_bc, b_bc, st, junk):
            A(lambda e: e.activation(junk[:], src, AF.Square, accum_out=st[:, 0:1]), r=srcdeps, w=[junk, st])
            V(lambda e: e.reduce_sum(out=st[:, 1:2], in_=src, axis=AX.X), r=srcdeps, w=[st])
            yield
            V(lambda e: e.tensor_scalar(st[:, 1:2], st[:, 1:2], 1.0 / D, None, op0=ALU.mult), r=[st], w=[st])
            V(lambda e: e.tensor_tensor(st[:, 2:3], st[:, 1:2], st[:, 1:2], op=ALU.mult), r=[st], w=[st])
            yield
            V(lambda e: e.scalar_tensor_tensor(out=st[:, 3:4], in0=st[:, 0:1], scalar=1.0 / D, in1=st[:, 2:3], op0=ALU.mult, op1=ALU.subtract), r=[st], w=[st])
            V(lambda e: e.tensor_scalar(st[:, 3:4], st[:, 3:4], cst[:, 0:1], None, op0=ALU.add), r=[st, cst], w=[st])
            yield
            G(lambda e: e.tensor_tensor(st[:, 3:4], st[:, 3:4], mhalf[:, 3:4], op=ALU.pow), r=[st, mhalf], w=[st])
            V(lambda e: e.scalar_tensor_tensor(out=st[:, 4:5], in0=st[:, 1:2], scalar=-1.0, in1=st[:, 3:4], op0=ALU.mult, op1=ALU.mult), r=[st], w=[st])
            yield
            A(lambda e: e.activation(dst[:], src, AF.Identity, bias=st[:, 4:5], scale=st[:, 3:4]), r=list(srcdeps) + [st], w=[dst])
            V(lambda e: e.tensor_tensor(dst[:], dst[:], g_bc[:], op=ALU.mult), r=[dst, g_bc], w=[dst])
            yield
            G(lambda e: e.tensor_tensor(dst[:], dst[:], b_bc[:], op=ALU.add), r=[dst, b_bc], w=[dst])
            yield

        with ExitStack() as e1:
            w_att = T(nc, e1, "w_att", [128, 8, 416], BF16)
            w_hy = T(nc, e1, "w_hy", [128, 8, 1536], BF16)
            w_uq_b = T(nc, e1, "w_uq_b", [128, 2, 768], BF16)
            w_ukv_b = T(nc, e1, "w_ukv_b", [128, 1024], BF16)
            kb.dma("pool", lambda e: e.dma_start(out=w_att[:], in_=w_in[:, 0:416].rearrange("(k p) n -> p k n", p=128)), writes=[w_att.b])
            kb.dma("pool", lambda e: e.dma_start(out=w_hy[:], in_=w_in[:, 416:1952].rearrange("(k p) n -> p k n", p=128)), writes=[w_hy.b])
            kb.dma("pool", lambda e: e.dma_start(out=w_uq_b[:], in_=w_uq.rearrange("(k p) n -> p k n", p=128)), writes=[w_uq_b.b])
            kb.dma("pool", lambda e: e.dma_start(out=w_ukv_b[:], in_=w_ukv), writes=[w_ukv_b.b])
            g_emb = load_bc(e1, "g_emb", emb_g, D); b_emb = load_bc(e1, "b_emb", emb_b, D)
            g_q = load_bc(e1, "g_q", q_norm_g, 256); g_kv = load_bc(e1, "g_kv", kv_norm_g, 128)
            shw = load_bc(e1, "shw", hy_short_w, 3 * 1536); shb = load_bc(e1, "shb", hy_short_b, 1536)
            hT = T(nc, e1, "hT", [128, 8, S + 2], BF16)
            V(lambda e: e.memset(hT[:, :, 0:1], 0.0), w=[hT])
            V(lambda e: e.memset(hT[:, :, S + 1:S + 2], 0.0), w=[hT])
            xt = [T(nc, e1, "xt%d" % i, [128, D], F32) for i in range(2)]
            ht = [T(nc, e1, "ht%d" % i, [128, D], F32) for i in range(2)]
            junk = T(nc, e1, "junk1", [128, D], F32)
            st = T(nc, e1, "st1", [128, 8], F32)
            apsb = T(nc, e1, "apsb", [128, 416], F32)
            cqn = T(nc, e1, "cqn", [128, 384], F32)
            cT = T(nc, e1, "cT", [128, 3, 128], BF16)
            Qt = T(nc, e1, "Qt", [128, 8, 96], F32)
            Kt = T(nc, e1, "Kt", [128, 8, 96], F32)
            Vt = T(nc, e1, "Vt", [128, 8, 65], BF16)
            V(lambda e: e.memset(Vt[:], 1.0), w=[Vt])
            rp = T(nc, e1, "rp", [128, 4, 8, 16], F32)
            kpe = T(nc, e1, "kpe", [128, 32], F32)
            csa = T(nc, e1, "csa", [128, NT, 32], F32)
            QTst = [T(nc, e1, "QTst%d" % i, [96, 8, 512], BF16) for i in range(1)]
            KTst = [T(nc, e1, "KTst%d" % i, [96, 8, 512], BF16) for i in range(1)]
            ut = [T(nc, e1, "ut%d" % i, [128, 1536], BF16) for i in range(2)]
            hyt = [T(nc, e1, "hyt%d" % i, [128, 512], F32) for i in range(3)]

            for b in range(NB):
                DMA("sp", xt[0][:], x[b, 0:128, :], w=[xt[0]])
                DMA("sp", csa[:], c_rope.rearrange("(nt p) c -> p nt c", p=128), w=[csa])
                for i in range(NT):
                    xx = xt[i % 2]; hh = ht[i % 2]
                    if i + 1 < NT:
                        DMA("sp", xt[(i + 1) % 2][:], x[b, (i + 1) * 128:(i + 2) * 128, :], w=[xt[(i + 1) % 2]])
                    layer_norm(xx[:], [xx], hh, g_emb, b_emb, st, junk)
                    DMA("sp", Hs[b, i * 128:(i + 1) * 128, :], hh[:], r=[hh], w=[bHs[b][i]])
                    for k in range(8):
                        pp = PS[4 + k // 4]
                        PE(lambda e: e.transpose(pp[:, (k % 4) * 128:(k % 4 + 1) * 128], hh[:, k * 128:(k + 1) * 128], ident[:]), r=[hh, ident], w=[pp], acc=True)
                    A(lambda e: e.copy(hT[:, 0:4, 1 + i * 128:1 + (i + 1) * 128], PS[4][:].rearrange("p (k t) -> p k t", k=4)), r=[PS[4]], w=[hT])
                    V(lambda e: e.tensor_copy(hT[:, 4:8, 1 + i * 128:1 + (i + 1) * 128], PS[5][:].rearrange("p (k t) -> p k t", k=4)), r=[PS[5]], w=[hT])
                for i in range(NT):
                    t0 = i * 128
                    g4 = i // 4; tl = i % 4
                    qs = QTst[0]; ks = KTst[0]
                    for k in range(8):
                        PE(lambda e: e.matmul(PS[0][:, 0:416], lhsT=hT[:, k, 1 + t0:1 + t0 + 128], rhs=w_att[:, k, :], start=(k == 0), stop=(k == 7)), r=[hT, w_att], w=[PS[0]], acc=True)
                    A(lambda e: e.copy(apsb[:], PS[0][:, 0:416]), r=[PS[0]], w=[apsb])
                    uu = ut[i % 2]
                    for cg in range(3):
                        for s in range(3):
                            for k in range(8):
                                PE(lambda e: e.matmul(PS[1 + s][:, :], lhsT=hT[:, k, t0 + s:t0 + s + 128], rhs=w_hy[:, k, cg * 512:(cg + 1) * 512], start=(k == 0), stop=(k == 7)), r=[hT, w_hy], w=[PS[1 + s]], acc=True)
                        for s in range(3):
                            V(lambda e: e.tensor_tensor(hyt[s][:], PS[1 + s][:], shw[:, s * 1536 + cg * 512:s * 1536 + (cg + 1) * 512], op=ALU.mult), r=[PS[1 + s], shw], w=[hyt[s]])
                        G(lambda e: e.tensor_tensor(hyt[0][:], hyt[0][:], hyt[1][:], op=ALU.add), r=[hyt[0], hyt[1]], w=[hyt[0]])
                        G(lambda e: e.tensor_tensor(hyt[2][:], hyt[2][:], shb[:, cg * 512:(cg + 1) * 512], op=ALU.add), r=[hyt[2], shb], w=[hyt[2]])
                        G(lambda e: e.tensor_tensor(uu[:, cg * 512:(cg + 1) * 512], hyt[0][:], hyt[2][:], op=ALU.add), r=[hyt[0], hyt[2]], w=[uu])
                    DMA("sp", Us[b, t0:t0 + 128, :], uu[:], r=[uu], w=[bUs[b][i]])
                    A(lambda e: e.activation(junk[:, 0:256], apsb[:, 0:256], AF.Square, accum_out=st[:, 5:6]), r=[apsb], w=[junk, st])
                    rstd_from_ssq(st, slice(5, 6), slice(5, 6), 1.0 / 256, 1)
                    V(lambda e: e.scalar_tensor_tensor(out=cqn[:, 0:256], in0=apsb[:, 0:256], scalar=st[:, 5:6], in1=g_q[:], op0=ALU.mult, op1=ALU.mult), r=[apsb, st, g_q], w=[cqn])
                    A(lambda e: e.activation(junk[:, 0:128], apsb[:, 256:384], AF.Square, accum_out=st[:, 6:7]), r=[apsb], w=[junk, st])
                    rstd_from_ssq(st, slice(6, 7), slice(6, 7), 1.0 / 128, 1)
                    V(lambda e: e.scalar_tensor_tensor(out=cqn[:, 256:384], in0=apsb[:, 256:384], scalar=st[:, 6:7], in1=g_kv[:], op0=ALU.mult, op1=ALU.mult), r=[apsb, st, g_kv], w=[cqn])
                    for j in range(3):
                        PE(lambda e: e.transpose(PS[0][:, j * 128:(j + 1) * 128], cqn[:, j * 128:(j + 1) * 128], ident[:]), r=[cqn, ident], w=[PS[0]], acc=True)
                    A(lambda e: e.copy(cT[:], PS[0][:, 0:384].rearrange("p (k t) -> p k t", k=3)), r=[PS[0]], w=[cT])
                    for j in range(2):
                        PE(lambda e: e.matmul(PS[4][:, :], lhsT=cT[:, j, :], rhs=w_uq_b[:, j, 0:512], start=(j == 0), stop=(j == 1)), r=[cT, w_uq_b], w=[PS[4]], acc=True)
                    for j in range(2):
                        PE(lambda e: e.matmul(PS[5][:, 0:256], lhsT=cT[:, j, :], rhs=w_uq_b[:, j, 512:768], start=(j == 0), stop=(j == 1)), r=[cT, w_uq_b], w=[PS[5]], acc=True)
                    for j in range(2):
                        PE(lambda e: e.matmul(PS[6 + j][:, :], lhsT=cT[:, 2, :], rhs=w_ukv_b[:, j * 512:(j + 1) * 512], start=True, stop=True), r=[cT, w_ukv_b], w=[PS[6 + j]])
                    qsc = 96.0 ** -0.5
                    Qf = Qt[:].rearrange("p h f -> p (h f)")
                    A(lambda e: e.mul(Qf[:, 0:512], PS[4][:, :], qsc), r=[PS[4]], w=[Qt])
                    A(lambda e: e.mul(Qf[:, 512:768], PS[5][:, 0:256], qsc), r=[PS[5]], w=[Qt])
                    cosb = bcast(csa[:, i, 0:16], [128, 8, 16], 1); sinb = bcast(csa[:, i, 16:32], [128, 8, 16], 1)
                    x1 = Qt[:, :, 64:80]; x2 = Qt[:, :, 80:96]
                    V(lambda e: e.tensor_tensor(rp[:, 0], x1, cosb, op=ALU.mult), r=[Qt, csa], w=[rp])
                    V(lambda e: e.tensor_tensor(rp[:, 1], x2, sinb, op=ALU.mult), r=[Qt, csa], w=[rp])
                    V(lambda e: e.tensor_tensor(rp[:, 2], x1, sinb, op=ALU.mult), r=[Qt, csa], w=[rp])
                    V(lambda e: e.tensor_tensor(rp[:, 3], x2, cosb, op=ALU.mult), r=[Qt, csa], w=[rp])
                    V(lambda e: e.tensor_tensor(x1, rp[:, 0], rp[:, 1], op=ALU.subtract), r=[rp], w=[Qt])
                    V(lambda e: e.tensor_tensor(x2, rp[:, 2], rp[:, 3], op=ALU.add), r=[rp], w=[Qt])
                    for j in range(2):
                        kvv = PS[6 + j][:].rearrange("p (h f) -> p h f", h=4)
                        A(lambda e: e.copy(Kt[:, 4 * j:4 * j + 4, 0:64], kvv[:, :, 0:64]), r=[PS[6 + j]], w=[Kt])
                        V(lambda e: e.tensor_copy(Vt[:, 4 * j:4 * j + 4, 0:64], kvv[:, :, 64:128]), r=[PS[6 + j]], w=[Vt])
                    kr1 = apsb[:, 384:400]; kr2 = apsb[:, 400:416]
                    V(lambda e: e.tensor_tensor(rp[:, 0, 0], kr1, csa[:, i, 0:16], op=ALU.mult), r=[apsb, csa], w=[rp])
                    V(lambda e: e.tensor_tensor(rp[:, 1, 0], kr2, csa[:, i, 16:32], op=ALU.mult), r=[apsb, csa], w=[rp])
                    V(lambda e: e.tensor_tensor(rp[:, 2, 0], kr1, csa[:, i, 16:32], op=ALU.mult), r=[apsb, csa], w=[rp])
                    V(lambda e: e.tensor_tensor(rp[:, 3, 0], kr2, csa[:, i, 0:16], op=ALU.mult), r=[apsb, csa], w=[rp])
                    V(lambda e: e.tensor_tensor(kpe[:, 0:16], rp[:, 0, 0], rp[:, 1, 0], op=ALU.subtract), r=[rp], w=[kpe])
                    V(lambda e: e.tensor_tensor(kpe[:, 16:32], rp[:, 2, 0], rp[:, 3, 0], op=ALU.add), r=[rp], w=[kpe])
                    V(lambda e: e.tensor_copy(Kt[:, :, 64:96], bcast(kpe[:], [128, 8, 32], 1)), r=[kpe], w=[Kt])
                    DMA("sp", Vs[b, t0:t0 + 128, :], Vt[:].rearrange("p h f -> p (h f)"), r=[Vt], w=[bVs[b]])
                    for (src, stg, pa) in ((Qt, qs, 4), (Kt, ks, 6)):
                        for h in range(8):
                            pp = PS[pa + h // 4]
                            PE(lambda e: e.transpose(pp[0:96, (h % 4) * 128:(h % 4 + 1) * 128], src[:, h, :], ident[:]), r=[src, ident], w=[pp], acc=True)
                        A(lambda e: e.copy(stg[:, 0:4, tl * 128:(tl + 1) * 128], PS[pa][0:96, :].rearrange("p (k t) -> p k t", k=4)), r=[PS[pa]], w=[stg])
                        V(lambda e: e.tensor_copy(stg[:, 4:8, tl * 128:(tl + 1) * 128], PS[pa + 1][0:96, :].rearrange("p (k t) -> p k t", k=4)), r=[PS[pa + 1]], w=[stg])
                    if tl == 3:
                        DMA("sp", QTs[b, :, :, g4 * 512:(g4 + 1) * 512].rearrange("h f s -> f h s"), qs[:], r=[qs], w=[bQK[b][g4]])
                        DMA("sp", KTs[b, :, :, g4 * 512:(g4 + 1) * 512].rearrange("h f s -> f h s"), ks[:], r=[ks], w=[bQK[b][g4]])

        with ExitStack() as e2:
            kb.barrier()
            g_att = load_bc(e2, "g_att", attn_out_g, 512)
            Vaug = T(nc, e2, "Vaug", [128, NT, 8 * 65], BF16)
            Ab = T(nc, e2, "Ab", [128, NT, 512], BF16, nb=NG)
            QTh = [T(nc, e2, "QTh%d" % i, [96, S], BF16) for i in range(2)]
            KTh = [T(nc, e2, "KTh%d" % i, [96, S], BF16) for i in range(2)]
            PT = [T(nc, e2, "PT%d" % i, [128, 512], BF16) for i in range(5)]
            SR = [PS[0], PS[1], PS[2], PS[6], PS[7]]
            OT = T(nc, e2, "OT", [65, 512], F32)
            o_n = T(nc, e2, "o_n", [128, 4, 64], F32)
            o_sq = T(nc, e2, "o_sq", [128, 4, 64], F32)
            st2 = T(nc, e2, "st2", [128, 16], F32)
            for b in range(NB):
                DMA("sp", Vaug[:], Vs[b].rearrange("(nt p) c -> p nt c", p=128), r=[bVs[b]], w=[Vaug])
                for h in range(8):
                    qh = QTh[h % 2]; kh = KTh[h % 2]
                    DMA("sp", qh[:], QTs[b, h], r=bQK[b], w=[qh])
                    DMA("sp", kh[:], KTs[b, h], r=bQK[b], w=[kh])
                    for g in range(NG):
                        pO = PS[3 + g % 2]
                        cnt = [0]

                        def qk(kc):
                            ps = SR[kc % 5]
                            PE(lambda e: e.matmul(ps[:, :], lhsT=kh[:, kc * 128:(kc + 1) * 128], rhs=qh[:, g * 512:(g + 1) * 512], start=True, stop=True), r=[kh, qh], w=[ps])
                        for kc0 in range(min(3, NT)):
                            qk(kc0)
                        for kc in range(NT):
                            if kc + 3 < NT:
                                qk(kc + 3)
                            ps = SR[kc % 5]; pt = PT[kc % 5]
                            A(lambda e: e.activation(pt[:], ps[:, :], AF.Exp), r=[ps], w=[pt])
                            PE(lambda e: e.matmul(pO[0:65, :], lhsT=Vaug[:, kc, h * 65:(h + 1) * 65], rhs=pt[:], start=(kc == 0), stop=(kc == NT - 1)), r=[Vaug, pt], w=[pO], acc=True)
                        V(lambda e: e.tensor_copy(OT[:], pO[0:65, :]), r=[pO], w=[OT])
                        for j in range(4):
                            PE(lambda e: e.transpose(PS[5][:, j * 65:(j + 1) * 65], OT[0:65, j * 128:(j + 1) * 128], ident[0:65, 0:65]), r=[OT, ident], w=[PS[5]], acc=True)
                        p5 = PS[5][:, 0:260].rearrange("p (j f) -> p j f", j=4)
                        V(lambda e: e.reciprocal(st2[:, 0:4], p5[:, :, 64]), r=[PS[5]], w=[st2])
                        V(lambda e: e.tensor_tensor(o_n[:], p5[:, :, 0:64], bcast(st2[:, 0:4], [128, 4, 64], 2), op=ALU.mult), r=[PS[5], st2], w=[o_n])
                        G(lambda e: e.tensor_tensor(o_sq[:], o_n[:], o_n[:], op=ALU.mult), r=[o_n], w=[o_sq])
                        V(lambda e: e.tensor_reduce(out=st2[:, 4:8], in_=o_sq[:], axis=AX.X, op=ALU.add), r=[o_sq], w=[st2])
                        V(lambda e: e.tensor_scalar(st2[:, 4:8], st2[:, 4:8], 1.0 / 64, 1e-6, op0=ALU.mult, op1=ALU.add), r=[st2], w=[st2])
                        A(lambda e: e.activation(st2[:, 4:8], st2[:, 4:8], AF.Ln), r=[st2], w=[st2])
                        A(lambda e: e.activation(st2[:, 4:8], st2[:, 4:8], AF.Exp, scale=-0.5), r=[st2], w=[st2])
                        V(lambda e: e.tensor_tensor(o_n[:], o_n[:], bcast(st2[:, 4:8], [128, 4, 64], 2), op=ALU.mult), r=[o_n, st2], w=[o_n])
                        G(lambda e: e.tensor_tensor(Ab[:, g * 4:(g + 1) * 4, h * 64:(h + 1) * 64], o_n[:], bcast(g_att[:, h * 64:(h + 1) * 64], [128, 4, 64], 1), op=ALU.mult), r=[o_n, g_att], w=[Ab.bs[g]])
                DMA("sp", As[b].rearrange("(nt p) c -> p nt c", p=128), Ab[:], r=Ab.bs, w=[bAs[b]])

        with ExitStack() as e3:
            kb.barrier()
            absd = load_bc(e3, "absd", c_absd, 512)
            tneg = T(nc, e3, "tneg", [128, NT], F32)
            DMA("sp", tneg[:], c_tneg, w=[tneg])
            hyb = load_bc(e3, "hyb", hy_bias, 1024)
            g_hy = load_bc(e3, "g_hy", hy_out_g, 512)
            alt = T(nc, e3, "alt", [128, 2], BF16)
            altf = T(nc, e3, "altf", [128, 1], F32)
            G(lambda e: e.iota(iot[:, 0:1], pattern=[[0, 1]], base=0, channel_multiplier=1), r=[iot], w=[iot])
            V(lambda e: e.tensor_scalar(iot[:, 0:1], iot[:, 0:1], 1, None, op0=ALU.bitwise_and), r=[iot], w=[iot])
            V(lambda e: e.tensor_copy(altf[:], iot[:, 0:1]), r=[iot], w=[altf])
            V(lambda e: e.tensor_scalar(alt[:, 0:1], altf[:], -2.0, 1.0, op0=ALU.mult, op1=ALU.add), r=[altf], w=[alt])
            with ExitStack() as e3a:
                zT = T(nc, e3a, "zT", [33, S], F32)
                DMA("sp", zT[:], c_zT, w=[zT])
                w1 = T(nc, e3a, "fw1", [33, 64], F32); DMA("sp", w1[:], fw1, w=[w1])
                w2 = T(nc, e3a, "fw2", [64, 64], F32); DMA("sp", w2[:], fw2, w=[w2])
                w3 = T(nc, e3a, "fw3", [64, 2048], F32); DMA("sp", w3[:], fw3, w=[w3])
                fp = T(nc, e3a, "fp", [64, 4], F32)
                for ci, src in enumerate((fb1, ffr1, fb2, ffr2)):
                    DMA("sp", fp[:, ci:ci + 1], src, w=[fp])
                h1T = T(nc, e3a, "h1T", [64, S], F32)
                h2T = T(nc, e3a, "h2T", [64, S], F32)
                ya = T(nc, e3a, "ya", [64, 512], F32); yb = T(nc, e3a, "yb", [64, 512], F32); yc = T(nc, e3a, "yc", [64, 512], F32)

                def sin_layer(dst, wT, K, srcT, bc, fc):
                    for c in range(S // 512):
                        PE(lambda e: e.matmul(PS[0][0:64, :], lhsT=wT[0:K, 0:64], rhs=srcT[0:K, c * 512:(c + 1) * 512], start=True, stop=True), r=[wT, srcT], w=[PS[0]])
                        V(lambda e: e.tensor_scalar(ya[:], PS[0][0:64, :], fp[:, bc:bc + 1], fp[:, fc:fc + 1], op0=ALU.add, op1=ALU.mult), r=[PS[0], fp], w=[ya])
                        A(lambda e: e.activation(yb[:], ya[:], AF.Abs), r=[ya], w=[yb])
                        A(lambda e: e.activation(yc[:], ya[:], AF.Sin, scale=0.5), r=[ya], w=[yc])
                        A(lambda e: e.activation(yb[:], yb[:], AF.Sin, bias=cst[0:64, 2:3], scale=-0.5), r=[yb, cst], w=[yb])
                        V(lambda e: e.scalar_tensor_tensor(out=dst[:, c * 512:(c + 1) * 512], in0=yc[:], scalar=2.0, in1=yb[:], op0=ALU.mult, op1=ALU.mult), r=[yc, yb], w=[dst])
                sin_layer(h1T, w1, 33, zT, 0, 1)
                sin_layer(h2T, w2, 64, h1T, 2, 3)
                dec = [T(nc, e3a, "dec%d" % i, [128, 512], F32) for i in range(2)]
                hd_ = [T(nc, e3a, "hd%d" % i, [128, 512], F32) for i in range(2)]
                hab = [T(nc, e3a, "hab%d" % i, [128, 512], F32) for i in range(2)]
                invs = T(nc, e3a, "invs", [1, 2048], F32)
                invbc = T(nc, e3a, "invbc", [128, 2048], F32)
                hn = [T(nc, e3a, "hn%d" % i, [128, 512], F32) for i in range(2)]
                hsd = [T(nc, e3a, "hsd%d" % i, [128, 2, 512], BF16) for i in range(2)]

                def mkdec(tc):
                    d = dec[tc % 2]
                    A(lambda e: e.activation(d[:], absd[:], AF.Exp, scale=tneg[:, tc:tc + 1]), r=[absd, tneg], w=[d])
                    return d
                for tc in range(NT):
                    d = mkdec(tc)
                    for cg in range(4):
                        pp = PS[cg % 2]; hh = hd_[cg % 2]; ha = hab[cg % 2]
                        PE(lambda e: e.matmul(pp[:, :], lhsT=h2T[0:64, tc * 128:(tc + 1) * 128], rhs=w3[0:64, cg * 512:(cg + 1) * 512], start=True, stop=True), r=[h2T, w3], w=[pp])
                        V(lambda e: e.tensor_tensor(hh[:], pp[:, :], d[:], op=ALU.mult), r=[pp, d], w=[hh])
                        A(lambda e: e.activation(ha[:], hh[:], AF.Abs), r=[hh], w=[ha])
                        PE(lambda e: e.matmul(PS[4 + cg][0:1, :], lhsT=ones_f[:, 0:1], rhs=ha[:], start=(tc == 0), stop=(tc == NT - 1)), r=[ones_f, ha], w=[PS[4 + cg]], acc=True)
                for cg in range(4):
                    V(lambda e: e.tensor_scalar(invs[:, cg * 512:(cg + 1) * 512], PS[4 + cg][0:1, :], 1e-6, None, op0=ALU.add), r=[PS[4 + cg]], w=[invs])
                V(lambda e: e.reciprocal(invs[:], invs[:]), r=[invs], w=[invs])
                for cg in range(4):
                    PE(lambda e: e.matmul(PS[cg % 2][:, :], lhsT=ones_f[0:1, :], rhs=invs[0:1, cg * 512:(cg + 1) * 512], start=True, stop=True), r=[ones_f, invs], w=[PS[cg % 2]])
                    V(lambda e: e.tensor_copy(invbc[:, cg * 512:(cg + 1) * 512], PS[cg % 2][:, :]), r=[PS[cg % 2]], w=[invbc])
                for tc in range(NT):
                    d = mkdec(tc)
                    for o in range(2):
                        for dr in range(2):
                            cg = o * 2 + dr
                            pp = PS[cg % 2]
                            PE(lambda e: e.matmul(pp[:, :], lhsT=h2T[0:64, tc * 128:(tc + 1) * 128], rhs=w3[0:64, cg * 512:(cg + 1) * 512], start=True, stop=True), r=[h2T, w3], w=[pp])
                            V(lambda e: e.tensor_tensor(hn[dr][:], pp[:, :], d[:], op=ALU.mult), r=[pp, d], w=[hn[dr]])
                            G(lambda e: e.tensor_tensor(hn[dr][:], hn[dr][:], invbc[:, cg * 512:(cg + 1) * 512], op=ALU.mult), r=[hn[dr], invbc], w=[hn[dr]])
                        so = hsd[o]
                        G(lambda e: e.tensor_tensor(so[:, 0, :], hn[0][:], hn[1][:], op=ALU.add), r=[hn[0], hn[1]], w=[so])
                        V(lambda e: e.tensor_tensor(so[:, 1, :], hn[0][:], hn[1][:], op=ALU.subtract), r=[hn[0], hn[1]], w=[so])
                        DMA("sp", HSs[o, :, tc * 128:(tc + 1) * 128, :].rearrange("a p c -> p a c"), so[:], r=[so], w=[bHSs[o]])
            kb.barrier()
            tab = T(nc, e3, "tab", [128, 3, N2, 128], BF16, nb=3)
            tmp = [T(nc, e3, "tmp%d" % i, [128, 512], F32) for i in range(6)]
            tmpA = tmp[0:4]
            tmpB = [T(nc, e3, "tmpb%d" % i, [128, 512], F32) for i in range(4)]

            def load_fwd(j):
                u = j % 3
                kb.dma("sp", lambda e: e.dma_start(out=tab[:, u, 0:NT, :], in_=c_fwd[j]), writes=[tab.bs[u]])
                kb.dma("sp", lambda e: e.dma_start(out=tab[:, u, NT:N2, :], in_=c_fwd[NT + j]), writes=[tab.bs[u]])

            with ExitStack() as e3b:
                kb.barrier()
                HS = T(nc, e3b, "HS", [128, 2, NT, 512], BF16)
                hft = [T(nc, e3b, "hft%d" % i, [128, 2, 512], BF16) for i in range(2)]
                for o in range(2):
                    DMA("sp", HS[:, 0], HSs[o, 0].rearrange("(nt p) c -> p nt c", p=128), r=[bHSs[o]], w=[HS])
                    DMA("sp", HS[:, 1], HSs[o, 1].rearrange("(nt p) c -> p nt c", p=128), r=[bHSs[o]], w=[HS])
                    for j in range(NT):
                        s = j % 2
                        u = j % 3
                        if j == 0:
                            load_fwd(0)
                        if j + 1 < NT:
                            load_fwd(j + 1)
                        hf = hft[j % 2]
                        for tc in range(NT):
                            PE(lambda e: e.matmul(PS[2 * s][:, :], lhsT=tab[:, u, tc, :], rhs=HS[:, 0, tc, :], start=(tc == 0), stop=(tc == NT - 1)), r=[tab.bs[u], HS], w=[PS[2 * s]], acc=True)
                        for tc in range(NT):
                            PE(lambda e: e.matmul(PS[2 * s + 1][:, :], lhsT=tab[:, u, NT + tc, :], rhs=HS[:, 1, tc, :], start=(tc == 0), stop=(tc == NT - 1)), r=[tab.bs[u], HS], w=[PS[2 * s + 1]], acc=True)
                        A(lambda e: e.copy(hf[:, 0, :], PS[2 * s][:, :]), r=[PS[2 * s]], w=[hf])
                        V(lambda e: e.tensor_copy(hf[:, 1, :], PS[2 * s + 1][:, :]), r=[PS[2 * s + 1]], w=[hf])
                        if j == 0:
                            for tc in range(NT):
                                PE(lambda e: e.matmul(PS[6][0:1, :], lhsT=alt[:, 0:1], rhs=HS[:, 0, tc, :], start=(tc == 0), stop=(tc == NT - 1)), r=[alt, HS], w=[PS[6]], acc=True)
                            V(lambda e: e.tensor_copy(hf[0:1, 1, :], PS[6][0:1, :]), r=[PS[6]], w=[hf])
                        DMA("sp", Hfs[o, j], hf[:], r=[hf], w=[bHfs[o][j]])

            with ExitStack() as e3c:
                kb.barrier()
                zb = T(nc, e3c, "zb", [128, NT, 512], BF16)
                Yf = T(nc, e3c, "Yf", [128, N2, 512], BF16)
                hft = [T(nc, e3c, "hfu%d" % i, [128, 2, 512], BF16) for i in range(2)]
                gt = [T(nc, e3c, "gt%d" % i, [128, 512], BF16) for i in range(2)]
                yt = [T(nc, e3c, "yt%d" % i, [128, 512], BF16) for i in range(2)]
                st3 = T(nc, e3c, "st3", [128, 16], F32)
                for b in range(NB):
                    DMA("sp", zb[:], Us[b, :, 0:512].rearrange("(nt p) c -> p nt c", p=128), r=bUs[b], w=[zb])
                    for o in range(2):
                        for j in range(NT):
                            s = j % 2
                            u = j % 3
                            load_fwd(j)
                            hf = hft[j % 2]
                            tmq = tmpA if s == 0 else tmpB
                            DMA("sp", hf[:], Hfs[o, j], r=[bHfs[o][j]], w=[hf])
                            for tc in range(NT):
                                PE(lambda e: e.matmul(PS[2 * s][:, :], lhsT=tab[:, u, tc, :], rhs=zb[:, tc, :], start=(tc == 0), stop=(tc == NT - 1)), r=[tab.bs[u], zb], w=[PS[2 * s]], acc=True)
                            for tc in range(NT):
                                PE(lambda e: e.matmul(PS[2 * s + 1][:, :], lhsT=tab[:, u, NT + tc, :], rhs=zb[:, tc, :], start=(tc == 0), stop=(tc == NT - 1)), r=[tab.bs[u], zb], w=[PS[2 * s + 1]], acc=True)
                            V(lambda e: e.tensor_tensor(tmq[0][:], PS[2 * s + 0][:, :], hf[:, 0, :], op=ALU.mult), r=[PS[2 * s + 0], hf], w=[tmq[0]])
                            V(lambda e: e.tensor_tensor(tmq[1][:], PS[2 * s + 1][:, :], hf[:, 1, :], op=ALU.mult), r=[PS[2 * s + 1], hf], w=[tmq[1]])
                            V(lambda e: e.tensor_tensor(tmq[2][:], PS[2 * s + 0][:, :], hf[:, 1, :], op=ALU.mult), r=[PS[2 * s + 0], hf], w=[tmq[2]])
                            V(lambda e: e.tensor_tensor(tmq[3][:], PS[2 * s + 1][:, :], hf[:, 0, :], op=ALU.mult), r=[PS[2 * s + 1], hf], w=[tmq[3]])
                            G(lambda e: e.tensor_tensor(Yf[:, j, :], tmq[0][:], tmq[1][:], op=ALU.subtract), r=[tmq[0], tmq[1]], w=[Yf])
                            G(lambda e: e.tensor_tensor(Yf[:, NT + j, :], tmq[2][:], tmq[3][:], op=ALU.add), r=[tmq[2], tmq[3]], w=[Yf])
                            if j == 0:
                                A(lambda e: e.copy(Yf[0:1, 0, :], tmq[0][0:1, :]), r=[tmq[0]], w=[Yf])
                                A(lambda e: e.copy(Yf[0:1, NT, :], tmq[1][0:1, :]), r=[tmq[1]], w=[Yf])
                        def ld_inv(tc_):
                            kb.dma("sp", lambda e: e.dma_start(out=tab[:, tc_ % 3], in_=c_inv[tc_]), writes=[tab.bs[tc_ % 3]])
                            DMA("sp", gt[tc_ % 2][:], Us[b, tc_ * 128:(tc_ + 1) * 128, (1 + o) * 512:(2 + o) * 512], r=[bUs[b][tc_]], w=[gt[tc_ % 2]])
                        for tc in range(NT):
                            s = tc % 2
                            u = tc % 3
                            gg = gt[tc % 2]
                            if tc == 0:
                                ld_inv(0)
                            if tc + 1 < NT:
                                ld_inv(tc + 1)
                            for kc in range(N2):
                                PE(lambda e: e.matmul(PS[4 + s][:, :], lhsT=tab[:, u, kc, :], rhs=Yf[:, kc, :], start=(kc == 0), stop=(kc == N2 - 1)), r=[tab.bs[u], Yf], w=[PS[4 + s]], acc=True)
                            G(lambda e: e.tensor_tensor(tmp[4][:], zb[:, tc, :], hyb[:, o * 512:(o + 1) * 512], op=ALU.mult), r=[zb, hyb], w=[tmp[4]])
                            V(lambda e: e.tensor_tensor(tmp[4][:], PS[4 + s][:, :], tmp[4][:], op=ALU.add), r=[PS[4 + s], tmp[4]], w=[tmp[4]])
                            if o == 0:
                                G(lambda e: e.tensor_tensor(zb[:, tc, :], tmp[4][:], gg[:], op=ALU.mult), r=[tmp[4], gg], w=[zb])
                            else:
                                G(lambda e: e.tensor_tensor(tmp[5][:], tmp[4][:], gg[:], op=ALU.mult), r=[tmp[4], gg], w=[tmp[5]])
                                A(lambda e: e.activation(tmp[4][:], tmp[5][:], AF.Square), r=[tmp[5]], w=[tmp[4]])
                                V(lambda e: e.tensor_reduce(out=st3[:, 0:8], in_=tmp[4][:].rearrange("p (g f) -> p g f", g=8), axis=AX.X, op=ALU.add), r=[tmp[4]], w=[st3])
                                rstd_from_ssq(st3, slice(0, 8), slice(0, 8), 1.0 / 64, 1)
                                V(lambda e: e.tensor_tensor(tmp[5][:].rearrange("p (g f) -> p g f", g=8), tmp[5][:].rearrange("p (g f) -> p g f", g=8), bcast(st3[:, 0:8], [128, 8, 64], 2), op=ALU.mult), r=[tmp[5], st3], w=[tmp[5]])
                                yy = yt[tc % 2]
                                G(lambda e: e.tensor_tensor(yy[:], tmp[5][:], g_hy[:], op=ALU.mult), r=[tmp[5], g_hy], w=[yy])
                                DMA("sp", Ys[b, tc * 128:(tc + 1) * 128, :], yy[:], r=[yy], w=[bYs[b][tc]])

        with ExitStack() as e4:
            kb.barrier()
            w_o_b = T(nc, e4, "w_o_b", [128, 8, D], BF16)
            w_q_b = T(nc, e4, "w_q_b", [128, 8, D], BF16)
            kb.dma("pool", lambda e: e.dma_start(out=w_o_b[:], in_=w_o.rearrange("(k p) n -> p k n", p=128)), writes=[w_o_b.b])
            kb.dma("pool", lambda e: e.dma_start(out=w_q_b[:], in_=peer_wq.rearrange("(k p) n -> p k n", p=128)), writes=[w_q_b.b])
            g_mix = load_bc(e4, "g_mix", ln_mix_g, D); b_mix = load_bc(e4, "b_mix", ln_mix_b, D)
            keysBD = T(nc, e4, "keysBD", [128, 8, 256], BF16)
            V(lambda e: e.memset(keysBD[:], 0.0), w=[keysBD])
            knat = T(nc, e4, "knat", [128, 8, 2, 64], F32)
            DMA("sp", knat[:], peer_keys.rearrange("h p n d -> n h p d"), w=[knat])
            for h in range(8):
                PE(lambda e: e.transpose(PS[h % 2][:, 0:128], knat[:, h].rearrange("n p d -> n (p d)"), ident[:]), r=[knat, ident], w=[PS[h % 2]])
                A(lambda e: e.copy(keysBD[0:64, h, 0:128], PS[h % 2][0:64, 0:128]), r=[PS[h % 2]], w=[keysBD])
                V(lambda e: e.tensor_copy(keysBD[64:128, h, 128:256], PS[h % 2][64:128, 0:128]), r=[PS[h % 2]], w=[keysBD])
            ioc = T(nc, e4, "ioc", [128, 16, 128], I32)
            G(lambda e: e.iota(ioc[:], pattern=[[0, 16], [1, 128]], base=0, channel_multiplier=0), w=[ioc])
            cat_2 = [T(nc, e4, "cat_%d" % q_, [128, D], F32) for q_ in range(2)]
            catb_2 = [T(nc, e4, "catb_%d" % q_, [128, D], BF16) for q_ in range(2)]
            catT_2 = [T(nc, e4, "catT_%d" % q_, [128, 8, 128], BF16) for q_ in range(2)]
            hres_2 = [T(nc, e4, "hres_%d" % q_, [128, D], F32) for q_ in range(2)]
            r1_2 = [T(nc, e4, "r1_%d" % q_, [128, D], F32) for q_ in range(2)]
            h2_2 = [T(nc, e4, "h2_%d" % q_, [128, D], F32) for q_ in range(2)]
            h2T_2 = [T(nc, e4, "h2Tp_%d" % q_, [128, 8, 128], BF16) for q_ in range(2)]
            qpT_2 = [T(nc, e4, "qpT_%d" % q_, [128, 8, 128], BF16) for q_ in range(2)]
            sc_2 = [T(nc, e4, "sc_%d" % q_, [128, 16, 128], F32) for q_ in range(2)]
            sct_2 = [T(nc, e4, "sct_%d" % q_, [128, 16, 128], F32) for q_ in range(2)]
            sv_2 = [T(nc, e4, "sv_%d" % q_, [128, 16, 16], F32) for q_ in range(2)]
            si_2 = [T(nc, e4, "si_%d" % q_, [128, 16, 16], I32) for q_ in range(2)]
            cand_2 = [T(nc, e4, "cand_%d" % q_, [128, 8, 256], F32) for q_ in range(2)]
            eid_2 = [T(nc, e4, "eid_%d" % q_, [128, 8, 256], I32) for q_ in range(2)]
            best_2 = [T(nc, e4, "best_%d" % q_, [128, 8, 16], F32) for q_ in range(2)]
            eii_2 = [T(nc, e4, "eii_%d" % q_, [128, 128], I32) for q_ in range(2)]
            gat_2 = [T(nc, e4, "gat_%d" % q_, [128, 8, 16], F32) for q_ in range(2)]
            st4_2 = [T(nc, e4, "st4_%d" % q_, [128, 32], F32) for q_ in range(2)]
            eij_2 = [T(nc, e4, "eij_%d" % q_, [128, 2, 128], I32) for q_ in range(2)]
            ijg_2 = [T(nc, e4, "ijg_%d" % q_, [128, 3, 128], F32) for q_ in range(2)]
            ijgT_2 = [T(nc, e4, "ijgT_%d" % q_, [128, 3, 128], F32) for q_ in range(2)]
            def ld4(b_, i_):
                q_ = (b_ * NT + i_) % 2
                DMA("sp", catb_2[q_][:, 0:512], As[b_, i_ * 128:(i_ + 1) * 128, :], r=[bAs[b_]], w=[catb_2[q_]])
                DMA("sp", catb_2[q_][:, 512:1024], Ys[b_, i_ * 128:(i_ + 1) * 128, :], r=[bYs[b_][i_]], w=[catb_2[q_]])
                DMA("sp", hres_2[q_][:], Hs[b_, i_ * 128:(i_ + 1) * 128, :], r=[bHs[b_][i_]], w=[hres_2[q_]])
            def tile4(b, i):
                t0 = i * 128
                tp_ = (b * NT + i) % 2
                cat = cat_2[tp_]
                catb = catb_2[tp_]
                catT = catT_2[tp_]
                hres = hres_2[tp_]
                r1 = r1_2[tp_]
                h2 = h2_2[tp_]
                h2T = h2T_2[tp_]
                qpT = qpT_2[tp_]
                sc = sc_2[tp_]
                sct = sct_2[tp_]
                sv = sv_2[tp_]
                si = si_2[tp_]
                cand = cand_2[tp_]
                eid = eid_2[tp_]
                best = best_2[tp_]
                eii = eii_2[tp_]
                gat = gat_2[tp_]
                st4 = st4_2[tp_]
                eij = eij_2[tp_]
                ijg = ijg_2[tp_]
                ijgT = ijgT_2[tp_]
                ld4(b, i)
                A(lambda e: e.copy(cat[:], catb[:]), r=[catb], w=[cat])
                for k in range(8):
                    pp = PS[k // 4]
                    PE(lambda e: e.transpose(pp[:, (k % 4) * 128:(k % 4 + 1) * 128], cat[:, k * 128:(k + 1) * 128], ident[:]), r=[cat, ident], w=[pp], acc=True)
                A(lambda e: e.copy(catT[:, 0:4, :], PS[0][:].rearrange("p (k t) -> p k t", k=4)), r=[PS[0]], w=[catT])
                V(lambda e: e.tensor_copy(catT[:, 4:8, :], PS[1][:].rearrange("p (k t) -> p k t", k=4)), r=[PS[1]], w=[catT])
                yield
                for hf_ in range(2):
                    for k in range(8):
                        PE(lambda e: e.matmul(PS[2 + hf_][:, :], lhsT=catT[:, k, :], rhs=w_o_b[:, k, hf_ * 512:(hf_ + 1) * 512], start=(k == 0), stop=(k == 7)), r=[catT, w_o_b], w=[PS[2 + hf_]], acc=True)
                alpha = 2.0 ** 0.25
                for hf_ in range(2):
                    V(lambda e: e.scalar_tensor_tensor(out=r1[:, hf_ * 512:(hf_ + 1) * 512], in0=hres[:, hf_ * 512:(hf_ + 1) * 512], scalar=alpha, in1=PS[2 + hf_][:, :], op0=ALU.mult, op1=ALU.add), r=[hres, PS[2 + hf_]], w=[r1])
                layer_norm(r1[:], [r1], h2, g_mix, b_mix, st4, h2)
                yield
                for k in range(8):
                    pp = PS[k // 4]
                    PE(lambda e: e.transpose(pp[:, (k % 4) * 128:(k % 4 + 1) * 128], h2[:, k * 128:(k + 1) * 128], ident[:]), r=[h2, ident], w=[pp], acc=True)
                A(lambda e: e.copy(h2T[:, 0:4, :], PS[0][:].rearrange("p (k t) -> p k t", k=4)), r=[PS[0]], w=[h2T])
                V(lambda e: e.tensor_copy(h2T[:, 4:8, :], PS[1][:].rearrange("p (k t) -> p k t", k=4)), r=[PS[1]], w=[h2T])
                yield
                for h in range(8):
                    pp = PS[2 + h // 4]
                    for k in range(8):
                        PE(lambda e: e.matmul(pp[:, (h % 4) * 128:(h % 4 + 1) * 128], lhsT=w_q_b[:, k, h * 128:(h + 1) * 128], rhs=h2T[:, k, :], start=(k == 0), stop=(k == 7)), r=[w_q_b, h2T], w=[pp], acc=True)
                A(lambda e: e.copy(qpT[:, 0:4, :], PS[2][:].rearrange("p (k t) -> p k t", k=4)), r=[PS[2]], w=[qpT])
                V(lambda e: e.tensor_copy(qpT[:, 4:8, :], PS[3][:].rearrange("p (k t) -> p k t", k=4)), r=[PS[3]], w=[qpT])
                yield
                for h in range(8):
                    pp = PS[4 + h // 2]
                    PE(lambda e: e.matmul(pp[:, (h % 2) * 256:(h % 2 + 1) * 256], lhsT=qpT[:, h, :], rhs=keysBD[:, h, :], start=True, stop=True), r=[qpT, keysBD], w=[pp], acc=True)
                for q4 in range(4):
                    (A if q4 % 2 == 0 else V)(lambda e: (e.copy if q4 % 2 == 0 else e.tensor_copy)(sc[:, q4 * 4:(q4 + 1) * 4, :], PS[4 + q4][:].rearrange("p (g n) -> p g n", g=4)), r=[PS[4 + q4]], w=[sc])
                V(lambda e: e.tensor_scalar(sc[:].bitcast(I32), sc[:].bitcast(I32), -128, None, op0=ALU.bitwise_and), r=[sc], w=[sc])
                V(lambda e: e.tensor_tensor(sc[:].bitcast(I32), sc[:].bitcast(I32), ioc[:], op=ALU.bitwise_or), r=[sc, ioc], w=[sc])
                yield
                scp = sc[:]
                for g in range(16):
                    V(lambda e: e.max(out=sv[:, g, 0:8], in_=scp[:, g, :]), r=[sc], w=[sv])
                    V(lambda e: e.match_replace(out=sct[:, g, :], in_to_replace=sv[:, g, 0:8], in_values=scp[:, g, :], imm_value=NEG), r=[sc, sv], w=[sct])
                    V(lambda e: e.max(out=sv[:, g, 8:16], in_=sct[:, g, :]), r=[sct], w=[sv])
                    if g % 4 == 3:
                        yield
                V(lambda e: e.tensor_scalar(si[:], sv[:].bitcast(I32), 127, None, op0=ALU.bitwise_and), r=[sv], w=[si])
                sv4 = sv[:].rearrange("p (h q) k -> p h q k", q=2)
                si4 = si[:].rearrange("p (h q) k -> p h q k", q=2)
                c4 = cand[:].rearrange("p h (i j) -> p h i j", i=16)
                e4v = eid[:].rearrange("p h (i j) -> p h i j", i=16)
                V(lambda e: e.tensor_tensor(c4, bcast(sv4[:, :, 0, :], [128, 8, 16, 16], 3), bcast(sv4[:, :, 1, :], [128, 8, 16, 16], 2), op=ALU.add), r=[sv], w=[cand])
                V(lambda e: e.tensor_scalar(si4[:, :, 0, :], si4[:, :, 0, :], 7, None, op0=ALU.logical_shift_left), r=[si], w=[si])
                V(lambda e: e.tensor_tensor(e4v, bcast(si4[:, :, 0, :], [128, 8, 16, 16], 3), bcast(si4[:, :, 1, :], [128, 8, 16, 16], 2), op=ALU.bitwise_or), r=[si], w=[eid])
                V(lambda e: e.tensor_scalar(cand[:].bitcast(I32), cand[:].bitcast(I32), -16384, None, op0=ALU.bitwise_and), r=[cand], w=[cand])
                V(lambda e: e.tensor_tensor(cand[:].bitcast(I32), cand[:].bitcast(I32), eid[:], op=ALU.bitwise_or), r=[cand, eid], w=[cand])
                yield
                cp = cand[:]
                candt = sct[:].rearrange("p (h a) n -> p h (a n)", a=2)
                for h in range(8):
                    V(lambda e: e.max(out=best[:, h, 0:8], in_=cp[:, h, :]), r=[cand], w=[best])
                    V(lambda e: e.match_replace(out=candt[:, h, :], in_to_replace=best[:, h, 0:8], in_values=cp[:, h, :], imm_value=NEG), r=[cand, best], w=[sct])
                    V(lambda e: e.max(out=best[:, h, 8:16], in_=candt[:, h, :]), r=[sct], w=[best])
                    if h % 4 == 3:
                        yield
                V(lambda e: e.tensor_tensor(gat[:], best[:], bcast(best[:, :, 0], [128, 8, 16], 2), op=ALU.subtract), r=[best], w=[gat])
                A(lambda e: e.activation(gat[:], gat[:], AF.Exp), r=[gat], w=[gat])
                V(lambda e: e.tensor_reduce(out=st4[:, 8:16], in_=gat[:], axis=AX.X, op=ALU.add), r=[gat], w=[st4])
                V(lambda e: e.reciprocal(st4[:, 8:16], st4[:, 8:16]), r=[st4], w=[st4])
                V(lambda e: e.tensor_tensor(gat[:], gat[:], bcast(st4[:, 8:16], [128, 8, 16], 2), op=ALU.mult), r=[gat, st4], w=[gat])
                yield
                V(lambda e: e.tensor_scalar(eii[:], best[:].rearrange("p h k -> p (h k)").bitcast(I32), 16383, None, op0=ALU.bitwise_and), r=[best], w=[eii])
                V(lambda e: e.tensor_scalar(eij[:, 0, :], eii[:], 7, None, op0=ALU.arith_shift_right), r=[eii], w=[eij])
                V(lambda e: e.tensor_scalar(eij[:, 1, :], eii[:], 127, None, op0=ALU.bitwise_and), r=[eii], w=[eij])
                V(lambda e: e.tensor_copy(ijg[:, 0:2, :], eij[:]), r=[eij], w=[ijg])
                G(lambda e: e.tensor_copy(ijg[:, 2, :], gat[:].rearrange("p h k -> p (h k)")), r=[gat], w=[ijg])
                for c3 in range(3):
                    PE(lambda e: e.transpose(PS[6][:, c3 * 128:(c3 + 1) * 128], ijg[:, c3, :], ident[:]), r=[ijg, ident], w=[PS[6]], acc=True)
                A(lambda e: e.copy(ijgT[:], PS[6][:, 0:384].rearrange("p (c t) -> p c t", c=3)), r=[PS[6]], w=[ijgT])
                DMA("sp", IJG[b, i], ijgT[:], r=[ijgT], w=[bP5[b][i]])
                DMA("sp", H2s[b, t0:t0 + 128, :], h2[:], r=[h2], w=[bP5[b][i]])
                DMA("sp", H2Ts[b, i], h2T[:], r=[h2T], w=[bP5[b][i]])


            u_v = peer_u.rearrange("(i j) d -> j i d", j=128)
            v_v = peer_v.rearrange("(i j) d -> j i d", j=128)
            ur = [T(nc, e4, "ur%d" % i_, [128, D], F32) for i_ in range(2)]
            utl = [T(nc, e4, "utl%d" % i_, [128, 8, 128], BF16) for i_ in range(2)]
            vr = [T(nc, e4, "vr%d" % i_, [128, D], BF16) for i_ in range(2)]

            def prologue_gen():
                for j in range(128):
                    u_ = ur[j % 2]; ut_ = utl[j % 2]; v_ = vr[j % 2]
                    DMA("sp", u_[:], u_v[j], w=[u_])
                    for k in range(8):
                        pp = PS[k // 4]
                        PE(lambda e: e.transpose(pp[:, (k % 4) * 128:(k % 4 + 1) * 128], u_[:, k * 128:(k + 1) * 128], ident[:]), r=[u_, ident], w=[pp], acc=True)
                    A(lambda e: e.copy(ut_[:, 0:4, :], PS[0][:].rearrange("p (k t) -> p k t", k=4)), r=[PS[0]], w=[ut_])
                    A(lambda e: e.copy(ut_[:, 4:8, :], PS[1][:].rearrange("p (k t) -> p k t", k=4)), r=[PS[1]], w=[ut_])
                    DMA("sp", UTs[j], ut_[:], r=[ut_], w=[bUT])
                    kb.dma("pool", lambda e: e.dma_start(out=v_[:], in_=v_v[j]), writes=[v_.b])
                    DMA("sp", VBs[j], v_[:], r=[v_], w=[bVB])
                    yield

            pg = prologue_gen()
            gens = [tile4(b_, i_) for b_ in range(NB) for i_ in range(NT)]
            active = []
            gi = 0
            steps = 0
            while gi < len(gens) or active:
                if gi < len(gens) and (len(active) == 0 or (len(active) == 1 and steps >= 6)):
                    active.append(gens[gi]); gi += 1; steps = 0
                for g_ in list(active):
                    try:
                        next(g_)
                    except StopIteration:
                        active.remove(g_)
                steps += 1
                if steps % 2 == 0:
                    try:
                        next(pg)
                    except StopIteration:
                        pass
            for _ in pg:
                pass

        with ExitStack() as e5:
            kb.barrier()
            g_ffn = load_bc(e5, "g_ffn", ln_ffn_g, D); b_ffn = load_bc(e5, "b_ffn", ln_ffn_b, D)
            iof = T(nc, e5, "iof", [128, 128], F32)
            G(lambda e: e.iota(iot[:], pattern=[[1, 128]], base=0, channel_multiplier=0), r=[iot], w=[iot])
            V(lambda e: e.tensor_copy(iof[:], iot[:]), r=[iot], w=[iof])
            kb.barrier()
            Am = T(nc, e5, "Am", [128, 128, 128], BF16)
            Bm = T(nc, e5, "Bm", [128, 128, 128], BF16)
            ga = T(nc, e5, "ga", [128, 128, 256], BF16)
            uts = [T(nc, e5, "uts%d" % i_, [128, 2, 8, 128], BF16) for i_ in range(4)]
            vbs = [T(nc, e5, "vbs%d" % i_, [128, 2, D], BF16) for i_ in range(4)]
            gtm = [T(nc, e5, "gtm%d" % i_, [128, 2, 128], BF16) for i_ in range(2)]
            h2r = T(nc, e5, "h2r", [128, D], F32)
            h2Trs = [T(nc, e5, "h2Tr%d" % i_, [128, 8, 256], BF16) for i_ in range(2)]
            ijgrs = [T(nc, e5, "ijgr%d" % i_, [128, 2, 3, 128], F32) for i_ in range(2)]
            r5 = T(nc, e5, "r5", [128, D], F32)
            o5 = T(nc, e5, "o5", [128, D], F32)
            st5 = T(nc, e5, "st5", [128, 8], F32)
            alpha = 2.0 ** 0.25
            tiles = [(b_, i_) for b_ in range(NB) for i_ in range(NT)]
            NP = len(tiles) // 2
            iob = bcast(iof[:], [128, 128, 128], 1)

            def prep_loads(p_):
                for tt in range(2):
                    b_, i_ = tiles[2 * p_ + tt]
                    DMA("sp", ijgrs[p_ % 2][:, tt], IJG[b_, i_], r=[bP5[b_][i_]], w=[ijgrs[p_ % 2]])
                    DMA("sp", h2Trs[p_ % 2][:, :, tt * 128:(tt + 1) * 128], H2Ts[b_, i_], r=[bP5[b_][i_]], w=[h2Trs[p_ % 2]])

            def build_ab(p_, tt):
                ijgr = ijgrs[p_ % 2]
                for c8 in range(8):
                    ts_ = slice(c8 * 16, (c8 + 1) * 16)
                    iob16 = bcast(iof[:], [128, 16, 128], 1)
                    V(lambda e: e.tensor_tensor(Am[:, ts_, :], iob16, bcast(ijgr[:, tt, 0, ts_], [128, 16, 128], 2), op=ALU.is_equal), r=[iof, ijgr], w=[Am])
                    V(lambda e: e.tensor_tensor(Bm[:, ts_, :], iob16, bcast(ijgr[:, tt, 1, ts_], [128, 16, 128], 2), op=ALU.is_equal), r=[iof, ijgr], w=[Bm])
                    V(lambda e: e.tensor_tensor(Bm[:, ts_, :], Bm[:, ts_, :], bcast(ijgr[:, tt, 2, ts_], [128, 16, 128], 2), op=ALU.mult), r=[Bm, ijgr], w=[Bm])
                    yield

            def run_all(gen):
                for _ in gen:
                    pass

            def step(gen):
                if gen is not None:
                    try:
                        next(gen)
                    except StopIteration:
                        pass

            def g_phase(tt, first):
                for t4 in range(32):
                    pp = PS[2 + t4 % 2]
                    for tl in range(4):
                        t = t4 * 4 + tl
                        PE(lambda e: e.matmul(pp[:, tl * 128:(tl + 1) * 128], lhsT=Am[:, t, :], rhs=Bm[:, t, :], start=True, stop=True), r=[Am, Bm], w=[pp], acc=True)
                    gv = ga[:, :, tt * 128 + t4 * 4:tt * 128 + t4 * 4 + 4]
                    pv = pp[:].rearrange("p (t j) -> p j t", t=4)
                    if first:
                        if t4 % 2 == 0:
                            V(lambda e: e.tensor_copy(gv, pv), r=[pp], w=[ga])
                        else:
                            A(lambda e: e.copy(gv, pv), r=[pp], w=[ga])
                    else:
                        V(lambda e: e.tensor_tensor(gv, gv, pv, op=ALU.mult), r=[ga, pp], w=[ga])

            def ld_u(q_):
                DMA("sp", uts[q_ % 4][:], UTs[q_ * 2:(q_ + 1) * 2].rearrange("j p k i -> p j k i"), r=[bUT], w=[uts[q_ % 4]])

            def ld_v(q_):
                DMA("sp", vbs[q_ % 4][:], VBs[q_ * 2:(q_ + 1) * 2].rearrange("j p d -> p j d"), r=[bVB], w=[vbs[q_ % 4]])

            def epilogue(p_):
                for tt in range(2):
                    b, i = tiles[2 * p_ + tt]
                    t0 = i * 128
                    DMA("sp", h2r[:], H2s[b, t0:t0 + 128, :], r=[bP5[b][i]], w=[h2r])
                    for hf_ in range(2):
                        V(lambda e: e.scalar_tensor_tensor(out=r5[:, hf_ * 512:(hf_ + 1) * 512], in0=h2r[:, hf_ * 512:(hf_ + 1) * 512], scalar=alpha, in1=PS[4 + 2 * tt + hf_][:, :], op0=ALU.mult, op1=ALU.add), r=[h2r, PS[4 + 2 * tt + hf_]], w=[r5])
                        yield
                    for _ in layer_norm_gen(r5[:], [r5], o5, g_ffn, b_ffn, st5, o5):
                        yield
                    DMA("sp", out[b, t0:t0 + 128, :], o5[:], r=[o5])
                    yield

            prep_loads(0)
            run_all(build_ab(0, 0))
            g_phase(0, True)
            for p_ in range(NP):
                h2Tr = h2Trs[p_ % 2]
                gep = epilogue(p_ - 1) if p_ > 0 else None
                gen1 = build_ab(p_, 1)
                for q_ in range(3):
                    ld_u(q_)
                for c in range(64):
                    us = uts[c % 4]
                    if c + 3 < 64:
                        ld_u(c + 3)
                    pp = PS[c % 4]
                    for jj in range(2):
                        for k in range(8):
                            PE(lambda e: e.matmul(pp[:, jj * 256:(jj + 1) * 256], lhsT=us[:, jj, k, :], rhs=h2Tr[:, k, :], start=(k == 0), stop=(k == 7)), r=[us, h2Tr], w=[pp], acc=True)
                    pv = pp[:].rearrange("p (j t) -> p j t", j=2)
                    gt_ = gtm[c % 2]
                    A(lambda e: e.activation(gt_[:], pv[:, :, 0:128], AF.Gelu), r=[pp], w=[gt_])
                    A(lambda e: e.activation(ga[:, 2 * c:2 * c + 2, 128:256], pv[:, :, 128:256], AF.Gelu), r=[pp], w=[ga])
                    V(lambda e: e.tensor_tensor(ga[:, 2 * c:2 * c + 2, 0:128], ga[:, 2 * c:2 * c + 2, 0:128], gt_[:], op=ALU.mult), r=[ga, gt_], w=[ga])
                    if c % 6 == 5:
                        step(gen1)
                    else:
                        step(gep)
                run_all(gen1)
                if gep is not None:
                    run_all(gep)
                for q_ in range(3):
                    ld_v(q_)
                g_phase(1, False)
                gen0 = None
                if p_ + 1 < NP:
                    prep_loads(p_ + 1)
                    gen0 = build_ab(p_ + 1, 0)
                for c in range(64):
                    vs = vbs[c % 4]
                    if c + 3 < 64:
                        ld_v(c + 3)
                    for jj in range(2):
                        j = c * 2 + jj
                        for tt in range(2):
                            for hf_ in range(2):
                                PE(lambda e: e.matmul(PS[4 + 2 * tt + hf_][:, :], lhsT=ga[:, j, tt * 128:(tt + 1) * 128], rhs=vs[:, jj, hf_ * 512:(hf_ + 1) * 512], start=(j == 0), stop=(j == 127)), r=[ga, vs], w=[PS[4 + 2 * tt + hf_]], acc=True)
                    if c % 6 == 5:
                        step(gen0)
                if gen0 is not None:
                    run_all(gen0)
                    g_phase(0, True)
            run_all(epilogue(NP - 1))
        kb.drain()
    return nc


def _consts(S):
    NT = S // 128
    n = 2 * S
    t = np.arange(S)
    inv = (1.0 / (10000.0 ** (np.arange(0, 32, 2, dtype=np.float32) / np.float32(32)))).astype(np.float32)
    ang = t.astype(np.float32)[:, None] * inv[None, :]
    rope = np.concatenate([np.cos(ang), np.sin(ang)], 1).astype(np.float32)
    tl = np.linspace(0.0, 1.0, S, dtype=np.float32)[:, None]
    w = (np.float32(2.0 * math.pi) * np.arange(S, dtype=np.float32) / np.float32(S)).astype(np.float32)
    f = np.linspace(1e-4, 15, 16, dtype=np.float32)
    angz = w[:, None] * f[None, :]
    z = np.concatenate([tl, np.cos(angz), -np.sin(angz)], -1).astype(np.float32)
    zT = np.ascontiguousarray(z.T)
    tneg = np.ascontiguousarray((-tl[:, 0]).reshape(NT, 128).T).astype(np.float32)
    absd = np.abs(np.linspace(math.log(1e-2) / 1.5, math.log(1e-2) / 0.3, 512, dtype=np.float32))[None, :].astype(np.float32)
    k = np.arange(S)
    kt = (t[:, None].astype(np.int64) * k[None, :]) % n
    ph = kt * (2.0 * np.pi / n)
    sgn = np.where(t % 2 == 0, 1.0, -1.0)
    Fre = np.cos(ph)
    Fim = -np.sin(ph)
    Fim[:, 0] = sgn
    F = np.concatenate([Fre, Fim], 1)
    FWD = np.ascontiguousarray(F.reshape(NT, 128, 2 * NT, 128).transpose(2, 1, 0, 3)).astype(ml_dtypes.bfloat16)
    Gre = (2.0 / n) * Fre.T
    Gre[0, :] = 1.0 / n
    Gim = (2.0 / n) * (-np.sin(ph.T))
    Gim[0, :] = sgn / n
    Gm = np.concatenate([Gre, Gim], 0)
    INV = np.ascontiguousarray(Gm.reshape(2 * NT, 128, NT, 128).transpose(2, 1, 0, 3)).astype(ml_dtypes.bfloat16)
    return dict(c_rope=rope, c_zT=zT, c_tneg=tneg, c_absd=absd, c_fwd=FWD, c_inv=INV)


_W2D = {
    "emb_ln_g": (1, D), "emb_ln_b": (1, D), "w_in": (D, 1952), "q_norm_g": (1, 256), "w_uq": (256, 768),
    "kv_norm_g": (1, 128), "w_ukv": (128, 1024), "hy_short_w": (1, 3 * 1536), "hy_short_b": (1, 1536),
    "hy_filt_w1": (33, 64), "hy_filt_b1": (64, 1), "hy_filt_freq1": (64, 1), "hy_filt_w2": (64, 64),
    "hy_filt_b2": (64, 1), "hy_filt_freq2": (64, 1), "hy_filt_w3": (64, 2048), "hy_bias": (1, 1024),
    "attn_out_g": (1, 512), "hy_out_g": (1, 512), "w_o": (D, D), "ln_mix_g": (1, D), "ln_mix_b": (1, D),
    "peer_wq": (D, D), "peer_sub_keys": (8, 2, 128, 64), "peer_u": (16384, D), "peer_v": (16384, D),
    "ln_ffn_g": (1, D), "ln_ffn_b": (1, D),
}


def run(inputs, ncores, NB):
    x = np.asarray(inputs["x"], dtype=np.float32)
    B, S, _ = x.shape
    assert B == ncores * NB
    nc = build(S, NB)
    shared = _consts(S)
    for name, shp in _W2D.items():
        shared[name] = np.ascontiguousarray(np.asarray(inputs[name], dtype=np.float32).reshape(shp))
    in_maps = []
    for c in range(ncores):
        m = dict(shared)
        m["x"] = np.ascontiguousarray(x[c * NB:(c + 1) * NB])
        in_maps.append(m)
    res = run_bass_kernel_spmd(nc, in_maps, core_ids=list(range(ncores)))
    return np.concatenate([np.asarray(r["out"]) for r in res.results], axis=0).astype(np.float32)


def kernel(**inputs):
    return run(inputs, 8, 2)
```

```python
import math
import numpy as np
import ml_dtypes
import concourse.bass as bass
import concourse.mybir as mybir
from concourse.bass_utils import run_bass_kernel_spmd
from contextlib import ExitStack

F32 = mybir.dt.float32
BF16 = mybir.dt.bfloat16
I32 = mybir.dt.int32
ALU = mybir.AluOpType
AF = mybir.ActivationFunctionType
AX = mybir.AxisListType

NDS = 48
NSW = 8
D = 1024
NEG = -1.0e30


class Buf:
    __slots__ = ("w", "r")

    def __init__(self):
        self.w = None
        self.r = {}


class KB:
    def __init__(self, nc, es):
        self.nc = nc
        self.eng = {"pe": nc.tensor, "dve": nc.vector, "act": nc.scalar,
                    "pool": nc.gpsimd, "sp": nc.sync}
        self.sem = {n: es.enter_context(nc.semaphore("s_" + n)) for n in self.eng}
        self.cnt = {n: 0 for n in self.eng}
        self.seen = {n: {} for n in self.eng}
        self.dsem = [es.enter_context(nc.semaphore("d%d" % i)) for i in range(NDS)]
        self.dcnt = [0] * NDS
        self.dnext = 0
        self.dnext_sw = 0

    def _wait(self, e, tok):
        if tok is None:
            return
        key, val = tok
        if self.seen[e].get(key, 0) >= val:
            return
        self.seen[e][key] = val
        sem = self.sem[key] if isinstance(key, str) else self.dsem[key]
        self.eng[e].wait_ge(sem, val)

    def _deps(self, e, reads, writes, pe_acc=False):
        for b in reads:
            self._wait(e, b.w)
        for b in writes:
            if not (pe_acc and b.w is not None and b.w[0] == "pe"):
                self._wait(e, b.w)
            for k, v in b.r.items():
                self._wait(e, (k, v))

    def op(self, e, fn, reads=(), writes=(), pe_acc=False):
        self._deps(e, reads, writes, pe_acc)
        ins = fn(self.eng[e])
        self.cnt[e] += 1
        c = self.cnt[e]
        ins.then_inc(self.sem[e], 1)
        for b in reads:
            if b.r.get(e, 0) < c:
                b.r[e] = c
        for b in writes:
            b.w = (e, c)
            b.r = {}

    def dma(self, q, fn, reads=(), writes=()):
        if q == "pool":
            i = self.dnext_sw
            self.dnext_sw = (i + 1) % NSW
        else:
            i = NSW + self.dnext
            self.dnext = (self.dnext + 1) % (NDS - NSW)
        if self.dcnt[i] > 0:
            self._wait(q, (i, self.dcnt[i]))
        self._deps(q, reads, writes)
        ins = fn(self.eng[q])
        self.dcnt[i] += 16
        v = self.dcnt[i]
        ins.then_inc(self.dsem[i], 16)
        for b in reads:
            if b.r.get(i, 0) < v:
                b.r[i] = v
        for b in writes:
            b.w = (i, v)
            b.r = {}

    def barrier(self):
        for e in self.eng:
            for o in self.eng:
                if o != e and self.cnt[o] > 0:
                    self._wait(e, (o, self.cnt[o]))
            for i in range(NDS):
                if self.dcnt[i] > 0:
                    self._wait(e, (i, self.dcnt[i]))

    def drain(self):
        for i in range(NDS):
            if self.dcnt[i] > 0:
                self._wait("sp", (i, self.dcnt[i]))


class T:
    def __init__(self, nc, es, name, shape, dt, psum=False, nb=1):
        f = nc.psum_tensor if psum else nc.sbuf_tensor
        self.t = es.enter_context(f(name, list(shape), dt))
        self.bs = [Buf() for _ in range(nb)]
        self.b = self.bs[0]

    def __getitem__(self, k):
        return self.t[k]


def bcast(ap, shape, axis):
    return ap.unsqueeze(axis).to_broadcast(list(shape))


def build(S, NB):
    NT = S // 128
    NG = S // 512
    N2 = 2 * NT
    n = 2 * S
    nc = bass.Bass("TRN2", target_bir_lowering=False)

    def din(name, shape, dt=F32):
        return nc.dram_tensor(name, list(shape), dt, kind="ExternalInput").ap()

    x = din("x", [NB, S, D])
    emb_g = din("emb_ln_g", [1, D]); emb_b = din("emb_ln_b", [1, D])
    w_in = din("w_in", [D, 1952])
    q_norm_g = din("q_norm_g", [1, 256]); w_uq = din("w_uq", [256, 768])
    kv_norm_g = din("kv_norm_g", [1, 128]); w_ukv = din("w_ukv", [128, 1024])
    hy_short_w = din("hy_short_w", [1, 3 * 1536]); hy_short_b = din("hy_short_b", [1, 1536])
    fw1 = din("hy_filt_w1", [33, 64]); fb1 = din("hy_filt_b1", [64, 1]); ffr1 = din("hy_filt_freq1", [64, 1])
    fw2 = din("hy_filt_w2", [64, 64]); fb2 = din("hy_filt_b2", [64, 1]); ffr2 = din("hy_filt_freq2", [64, 1])
    fw3 = din("hy_filt_w3", [64, 2048])
    hy_bias = din("hy_bias", [1, 1024])
    attn_out_g = din("attn_out_g", [1, 512]); hy_out_g = din("hy_out_g", [1, 512])
    w_o = din("w_o", [D, D])
    ln_mix_g = din("ln_mix_g", [1, D]); ln_mix_b = din("ln_mix_b", [1, D])
    peer_wq = din("peer_wq", [D, D])
    peer_keys = din("peer_sub_keys", [8, 2, 128, 64])
    peer_u = din("peer_u", [16384, D]); peer_v = din("peer_v", [16384, D])
    ln_ffn_g = din("ln_ffn_g", [1, D]); ln_ffn_b = din("ln_ffn_b", [1, D])
    c_rope = din("c_rope", [S, 32])
    c_zT = din("c_zT", [33, S])
    c_tneg = din("c_tneg", [128, NT])
    c_absd = din("c_absd", [1, 512])
    c_fwd = din("c_fwd", [N2, 128, NT, 128], BF16)
    c_inv = din("c_inv", [NT, 128, N2, 128], BF16)
    out = nc.dram_tensor("out", [NB, S, D], F32, kind="ExternalOutput").ap()

    def dscr(name, shape, dt):
        return nc.dram_tensor(name, list(shape), dt).ap()

    Hs = dscr("Hs", [NB, S, D], F32)
    QTs = dscr("QTs", [NB, 8, 96, S], BF16)
    KTs = dscr("KTs", [NB, 8, 96, S], BF16)
    Vs = dscr("Vs", [NB, S, 8 * 65], BF16)
    Us = dscr("Us", [NB, S, 1536], BF16)
    As = dscr("As", [NB, S, 512], BF16)
    Ys = dscr("Ys", [NB, S, 512], BF16)
    HSs = dscr("HSs", [2, 2, S, 512], BF16)
    Hfs = dscr("Hfs", [2, NT, 128, 2, 512], BF16)
    bHs = [[Buf() for _ in range(NT)] for _ in range(NB)]
    bQK = [[Buf() for _ in range(NG)] for _ in range(NB)]
    bVs = [Buf() for _ in range(NB)]
    bUs = [[Buf() for _ in range(NT)] for _ in range(NB)]
    bAs = [Buf() for _ in range(NB)]
    bYs = [[Buf() for _ in range(NT)] for _ in range(NB)]
    bHSs = [Buf() for _ in range(2)]
    bHfs = [[Buf() for _ in range(NT)] for _ in range(2)]
    H2s = dscr("H2s", [NB, S, D], F32)
    H2Ts = dscr("H2Ts", [NB, NT, 128, 8, 128], BF16)
    IJG = dscr("IJG", [NB, NT, 128, 3, 128], F32)
    UTs = dscr("UTs", [128, 128, 8, 128], BF16)
    VBs = dscr("VBs", [128, 128, D], BF16)
    bP5 = [[Buf() for _ in range(NT)] for _ in range(NB)]
    bUT = Buf(); bVB = Buf()

    with ExitStack() as es:
        kb = KB(nc, es)

        def V(fn, r=(), w=()):
            kb.op("dve", fn, [t.b if isinstance(t, T) else t for t in r], [t.b if isinstance(t, T) else t for t in w])

        def A(fn, r=(), w=()):
            kb.op("act", fn, [t.b if isinstance(t, T) else t for t in r], [t.b if isinstance(t, T) else t for t in w])

        def G(fn, r=(), w=()):
            kb.op("pool", fn, [t.b if isinstance(t, T) else t for t in r], [t.b if isinstance(t, T) else t for t in w])

        def PE(fn, r=(), w=(), acc=False):
            kb.op("pe", fn, [t.b if isinstance(t, T) else t for t in r], [t.b if isinstance(t, T) else t for t in w], pe_acc=acc)

        def DMA(q, o, i, r=(), w=()):
            kb.dma(q, lambda e: e.dma_start(out=o, in_=i), [t.b if isinstance(t, T) else t for t in r], [t.b if isinstance(t, T) else t for t in w])

        PS = [T(nc, es, "ps%d" % i, [128, 512], F32, psum=True) for i in range(8)]
        ident = T(nc, es, "ident", [128, 128], F32)
        iot = T(nc, es, "iot", [128, 128], I32)
        G(lambda e: e.iota(iot[:], pattern=[[1, 128]], base=0, channel_multiplier=-1), w=[iot])
        V(lambda e: e.tensor_scalar(ident[:], iot[:], 0, None, op0=ALU.is_equal), r=[iot], w=[ident])
        ones_f = T(nc, es, "ones_f", [128, 128], F32)
        V(lambda e: e.memset(ones_f[:], 1.0), w=[ones_f])
        cst = T(nc, es, "cst", [128, 4], F32)
        V(lambda e: e.memset(cst[:, 0:1], 1e-5), w=[cst])
        V(lambda e: e.memset(cst[:, 1:2], 1e-6), w=[cst])
        V(lambda e: e.memset(cst[:, 2:3], math.pi / 2), w=[cst])

        mhalf = T(nc, es, "mhalf", [128, 32], F32)
        V(lambda e: e.memset(mhalf[:], -0.5), w=[mhalf])

        def load_bc(es_, name, src, width):
            t = T(nc, es_, name, [128, width], F32)
            DMA("sp", t[:], src.partition_broadcast(128), w=[t])
            return t

        def rstd_from_ssq(st, col_in, col_out, inv_n, eps_col):
            V(lambda e: e.tensor_scalar(st[:, col_out], st[:, col_in], inv_n, cst[:, eps_col:eps_col + 1], op0=ALU.mult, op1=ALU.add), r=[st, cst], w=[st])
            G(lambda e: e.tensor_tensor(st[:, col_out], st[:, col_out], mhalf[:, col_out], op=ALU.pow), r=[st, mhalf], w=[st])

        def layer_norm(src, srcdeps, dst, g_bc, b_bc, st, junk):
            A(lambda e: e.activation(junk[:], src, AF.Square, accum_out=st[:, 0:1]), r=srcdeps, w=[junk, st])
            V(lambda e: e.reduce_sum(out=st[:, 1:2], in_=src, axis=AX.X), r=srcdeps, w=[st])
            V(lambda e: e.tensor_scalar(st[:, 1:2], st[:, 1:2], 1.0 / D, None, op0=ALU.mult), r=[st], w=[st])
            V(lambda e: e.tensor_tensor(st[:, 2:3], st[:, 1:2], st[:, 1:2], op=ALU.mult), r=[st], w=[st])
            V(lambda e: e.scalar_tensor_tensor(out=st[:, 3:4], in0=st[:, 0:1], scalar=1.0 / D, in1=st[:, 2:3], op0=ALU.mult, op1=ALU.subtract), r=[st], w=[st])
            V(lambda e: e.tensor_scalar(st[:, 3:4], st[:, 3:4], cst[:, 0:1], None, op0=ALU.add), r=[st, cst], w=[st])
            G(lambda e: e.tensor_tensor(st[:, 3:4], st[:, 3:4], mhalf[:, 3:4], op=ALU.pow), r=[st, mhalf], w=[st])
            V(lambda e: e.scalar_tensor_tensor(out=st[:, 4:5], in0=st[:, 1:2], scalar=-1.0, in1=st[:, 3:4], op0=ALU.mult, op1=ALU.mult), r=[st], w=[st])
            A(lambda e: e.activation(dst[:], src, AF.Identity, bias=st[:, 4:5], scale=st[:, 3:4]), r=list(srcdeps) + [st], w=[dst])
            V(lambda e: e.tensor_tensor(dst[:], dst[:], g_bc[:], op=ALU.mult), r=[dst, g_bc], w=[dst])
            G(lambda e: e.tensor_tensor(dst[:], dst[:], b_bc[:], op=ALU.add), r=[dst, b_bc], w=[dst])

        def layer_norm_gen(src, srcdeps, dst, g_bc, b_bc, st, junk):
            A(lambda e: e.activation(junk[:], src, AF.Square, accum_out=st[:, 0:1]), r=srcdeps, w=[junk, st])
            V(lambda e: e.reduce_sum(out=st[:, 1:2], in_=src, axis=AX.X), r=srcdeps, w=[st])
            yield
            V(lambda e: e.tensor_scalar(st[:, 1:2], st[:, 1:2], 1.0 / D, None, op0=ALU.mult), r=[st], w=[st])
            V(lambda e: e.tensor_tensor(st[:, 2:3], st[:, 1:2], st[:, 1:2], op=ALU.mult), r=[st], w=[st])
            yield
            V(lambda e: e.scalar_tensor_tensor(out=st[:, 3:4], in0=st[:, 0:1], scalar=1.0 / D, in1=st[:, 2:3], op0=ALU.mult, op1=ALU.subtract), r=[st], w=[st])
            V(lambda e: e.tensor_scalar(st[:, 3:4], st[:, 3:4], cst[:, 0:1], None, op0=ALU.add), r=[st, cst], w=[st])
            yield
            G(lambda e: e.tensor_tensor(st[:, 3:4], st[:, 3:4], mhalf[:, 3:4], op=ALU.pow), r=[st, mhalf], w=[st])
            V(lambda e: e.scalar_tensor_tensor(out=st[:, 4:5], in0=st[:, 1:2], scalar=-1.0, in1=st[:, 3:4], op0=ALU.mult, op1=ALU.mult), r=[st], w=[st])
            yield
            A(lambda e: e.activation(dst[:], src, AF.Identity, bias=st[:, 4:5], scale=st[:, 3:4]), r=list(srcdeps) + [st], w=[dst])
            V(lambda e: e.tensor_tensor(dst[:], dst[:], g_bc[:], op=ALU.mult), r=[dst, g_bc], w=[dst])
            yield
            G(lambda e: e.tensor_tensor(dst[:], dst[:], b_bc[:], op=ALU.add), r=[dst, b_bc], w=[dst])
            yield

        with ExitStack() as e1:
            w_att = T(nc, e1, "w_att", [128, 8, 416], BF16)
            w_hy = T(nc, e1, "w_hy", [128, 8, 1536], BF16)
            w_uq_b = T(nc, e1, "w_uq_b", [128, 2, 768], BF16)
            w_ukv_b = T(nc, e1, "w_ukv_b", [128, 1024], BF16)
            kb.dma("pool", lambda e: e.dma_start(out=w_att[:], in_=w_in[:, 0:416].rearrange("(k p) n -> p k n", p=128)), writes=[w_att.b])
            kb.dma("pool", lambda e: e.dma_start(out=w_hy[:], in_=w_in[:, 416:1952].rearrange("(k p) n -> p k n", p=128)), writes=[w_hy.b])
            kb.dma("pool", lambda e: e.dma_start(out=w_uq_b[:], in_=w_uq.rearrange("(k p) n -> p k n", p=128)), writes=[w_uq_b.b])
            kb.dma("pool", lambda e: e.dma_start(out=w_ukv_b[:], in_=w_ukv), writes=[w_ukv_b.b])
            g_emb = load_bc(e1, "g_emb", emb_g, D); b_emb = load_bc(e1, "b_emb", emb_b, D)
            g_q = load_bc(e1, "g_q", q_norm_g, 256); g_kv = load_bc(e1, "g_kv", kv_norm_g, 128)
            shw = load_bc(e1, "shw", hy_short_w, 3 * 1536); shb = load_bc(e1, "shb", hy_short_b, 1536)
            hT = T(nc, e1, "hT", [128, 8, S + 2], BF16)
            V(lambda e: e.memset(hT[:, :, 0:1], 0.0), w=[hT])
            V(lambda e: e.memset(hT[:, :, S + 1:S + 2], 0.0), w=[hT])
            xt = [T(nc, e1, "xt%d" % i, [128, D], F32) for i in range(2)]
            ht = [T(nc, e1, "ht%d" % i, [128, D], F32) for i in range(2)]
            junk = T(nc, e1, "junk1", [128, D], F32)
            st = T(nc, e1, "st1", [128, 8], F32)
            apsb = T(nc, e1, "apsb", [128, 416], F32)
            cqn = T(nc, e1, "cqn", [128, 384], F32)
            cT = T(nc, e1, "cT", [128, 3, 128], BF16)
            Qt = T(nc, e1, "Qt", [128, 8, 96], F32)
            Kt = T(nc, e1, "Kt", [128, 8, 96], F32)
            Vt = T(nc, e1, "Vt", [128, 8, 65], BF16)
            V(lambda e: e.memset(Vt[:], 1.0), w=[Vt])
            rp = T(nc, e1, "rp", [128, 4, 8, 16], F32)
            kpe = T(nc, e1, "kpe", [128, 32], F32)
            csa = T(nc, e1, "csa", [128, NT, 32], F32)
            QTst = [T(nc, e1, "QTst%d" % i, [96, 8, 512], BF16) for i in range(1)]
            KTst = [T(nc, e1, "KTst%d" % i, [96, 8, 512], BF16) for i in range(1)]
            ut = [T(nc, e1, "ut%d" % i, [128, 1536], BF16) for i in range(2)]
            hyt = [T(nc, e1, "hyt%d" % i, [128, 512], F32) for i in range(3)]

            for b in range(NB):
                DMA("sp", xt[0][:], x[b, 0:128, :], w=[xt[0]])
                DMA("sp", csa[:], c_rope.rearrange("(nt p) c -> p nt c", p=128), w=[csa])
                for i in range(NT):
                    xx = xt[i % 2]; hh = ht[i % 2]
                    if i + 1 < NT:
                        DMA("sp", xt[(i + 1) % 2][:], x[b, (i + 1) * 128:(i + 2) * 128, :], w=[xt[(i + 1) % 2]])
                    layer_norm(xx[:], [xx], hh, g_emb, b_emb, st, junk)
                    DMA("sp", Hs[b, i * 128:(i + 1) * 128, :], hh[:], r=[hh], w=[bHs[b][i]])
                    for k in range(8):
                        pp = PS[4 + k // 4]
                        PE(lambda e: e.transpose(pp[:, (k % 4) * 128:(k % 4 + 1) * 128], hh[:, k * 128:(k + 1) * 128], ident[:]), r=[hh, ident], w=[pp], acc=True)
                    A(lambda e: e.copy(hT[:, 0:4, 1 + i * 128:1 + (i + 1) * 128], PS[4][:].rearrange("p (k t) -> p k t", k=4)), r=[PS[4]], w=[hT])
                    V(lambda e: e.tensor_copy(hT[:, 4:8, 1 + i * 128:1 + (i + 1) * 128], PS[5][:].rearrange("p (k t) -> p k t", k=4)), r=[PS[5]], w=[hT])
                for i in range(NT):
                    t0 = i * 128
                    g4 = i // 4; tl = i % 4
                    qs = QTst[0]; ks = KTst[0]
                    for k in range(8):
                        PE(lambda e: e.matmul(PS[0][:, 0:416], lhsT=hT[:, k, 1 + t0:1 + t0 + 128], rhs=w_att[:, k, :], start=(k == 0), stop=(k == 7)), r=[hT, w_att], w=[PS[0]], acc=True)
                    A(lambda e: e.copy(apsb[:], PS[0][:, 0:416]), r=[PS[0]], w=[apsb])
                    uu = ut[i % 2]
                    for cg in range(3):
                        for s in range(3):
                            for k in range(8):
                                PE(lambda e: e.matmul(PS[1 + s][:, :], lhsT=hT[:, k, t0 + s:t0 + s + 128], rhs=w_hy[:, k, cg * 512:(cg + 1) * 512], start=(k == 0), stop=(k == 7)), r=[hT, w_hy], w=[PS[1 + s]], acc=True)
                        for s in range(3):
                            V(lambda e: e.tensor_tensor(hyt[s][:], PS[1 + s][:], shw[:, s * 1536 + cg * 512:s * 1536 + (cg + 1) * 512], op=ALU.mult), r=[PS[1 + s], shw], w=[hyt[s]])
                        G(lambda e: e.tensor_tensor(hyt[0][:], hyt[0][:], hyt[1][:], op=ALU.add), r=[hyt[0], hyt[1]], w=[hyt[0]])
                        G(lambda e: e.tensor_tensor(hyt[2][:], hyt[2][:], shb[:, cg * 512:(cg + 1) * 512], op=ALU.add), r=[hyt[2], shb], w=[hyt[2]])
                        G(lambda e: e.tensor_tensor(uu[:, cg * 512:(cg + 1) * 512], hyt[0][:], hyt[2][:], op=ALU.add), r=[hyt[0], hyt[2]], w=[uu])
                    DMA("sp", Us[b, t0:t0 + 128, :], uu[:], r=[uu], w=[bUs[b][i]])
                    A(lambda e: e.activation(junk[:, 0:256], apsb[:, 0:256], AF.Square, accum_out=st[:, 5:6]), r=[apsb], w=[junk, st])
                    rstd_from_ssq(st, slice(5, 6), slice(5, 6), 1.0 / 256, 1)
                    V(lambda e: e.scalar_tensor_tensor(out=cqn[:, 0:256], in0=apsb[:, 0:256], scalar=st[:, 5:6], in1=g_q[:], op0=ALU.mult, op1=ALU.mult), r=[apsb, st, g_q], w=[cqn])
                    A(lambda e: e.activation(junk[:, 0:128], apsb[:, 256:384], AF.Square, accum_out=st[:, 6:7]), r=[apsb], w=[junk, st])
                    rstd_from_ssq(st, slice(6, 7), slice(6, 7), 1.0 / 128, 1)
                    V(lambda e: e.scalar_tensor_tensor(out=cqn[:, 256:384], in0=apsb[:, 256:384], scalar=st[:, 6:7], in1=g_kv[:], op0=ALU.mult, op1=ALU.mult), r=[apsb, st, g_kv], w=[cqn])
                    for j in range(3):
                        PE(lambda e: e.transpose(PS[0][:, j * 128:(j + 1) * 128], cqn[:, j * 128:(j + 1) * 128], ident[:]), r=[cqn, ident], w=[PS[0]], acc=True)
                    A(lambda e: e.copy(cT[:], PS[0][:, 0:384].rearrange("p (k t) -> p k t", k=3)), r=[PS[0]], w=[cT])
                    for j in range(2):
                        PE(lambda e: e.matmul(PS[4][:, :], lhsT=cT[:, j, :], rhs=w_uq_b[:, j, 0:512], start=(j == 0), stop=(j == 1)), r=[cT, w_uq_b], w=[PS[4]], acc=True)
                    for j in range(2):
                        PE(lambda e: e.matmul(PS[5][:, 0:256], lhsT=cT[:, j, :], rhs=w_uq_b[:, j, 512:768], start=(j == 0), stop=(j == 1)), r=[cT, w_uq_b], w=[PS[5]], acc=True)
                    for j in range(2):
                        PE(lambda e: e.matmul(PS[6 + j][:, :], lhsT=cT[:, 2, :], rhs=w_ukv_b[:, j * 512:(j + 1) * 512], start=True, stop=True), r=[cT, w_ukv_b], w=[PS[6 + j]])
                    qsc = 96.0 ** -0.5
                    Qf = Qt[:].rearrange("p h f -> p (h f)")
                    A(lambda e: e.mul(Qf[:, 0:512], PS[4][:, :], qsc), r=[PS[4]], w=[Qt])
                    A(lambda e: e.mul(Qf[:, 512:768], PS[5][:, 0:256], qsc), r=[PS[5]], w=[Qt])
                    cosb = bcast(csa[:, i, 0:16], [128, 8, 16], 1); sinb = bcast(csa[:, i, 16:32], [128, 8, 16], 1)
                    x1 = Qt[:, :, 64:80]; x2 = Qt[:, :, 80:96]
                    V(lambda e: e.tensor_tensor(rp[:, 0], x1, cosb, op=ALU.mult), r=[Qt, csa], w=[rp])
                    V(lambda e: e.tensor_tensor(rp[:, 1], x2, sinb, op=ALU.mult), r=[Qt, csa], w=[rp])
                    V(lambda e: e.tensor_tensor(rp[:, 2], x1, sinb, op=ALU.mult), r=[Qt, csa], w=[rp])
                    V(lambda e: e.tensor_tensor(rp[:, 3], x2, cosb, op=ALU.mult), r=[Qt, csa], w=[rp])
                    V(lambda e: e.tensor_tensor(x1, rp[:, 0], rp[:, 1], op=ALU.subtract), r=[rp], w=[Qt])
                    V(lambda e: e.tensor_tensor(x2, rp[:, 2], rp[:, 3], op=ALU.add), r=[rp], w=[Qt])
                    for j in range(2):
                        kvv = PS[6 + j][:].rearrange("p (h f) -> p h f", h=4)
                        A(lambda e: e.copy(Kt[:, 4 * j:4 * j + 4, 0:64], kvv[:, :, 0:64]), r=[PS[6 + j]], w=[Kt])
                        V(lambda e: e.tensor_copy(Vt[:, 4 * j:4 * j + 4, 0:64], kvv[:, :, 64:128]), r=[PS[6 + j]], w=[Vt])
                    kr1 = apsb[:, 384:400]; kr2 = apsb[:, 400:416]
                    V(lambda e: e.tensor_tensor(rp[:, 0, 0], kr1, csa[:, i, 0:16], op=ALU.mult), r=[apsb, csa], w=[rp])
                    V(lambda e: e.tensor_tensor(rp[:, 1, 0], kr2, csa[:, i, 16:32], op=ALU.mult), r=[apsb, csa], w=[rp])
                    V(lambda e: e.tensor_tensor(rp[:, 2, 0], kr1, csa[:, i, 16:32], op=ALU.mult), r=[apsb, csa], w=[rp])
                    V(lambda e: e.tensor_tensor(rp[:, 3, 0], kr2, csa[:, i, 0:16], op=ALU.mult), r=[apsb, csa], w=[rp])
                    V(lambda e: e.tensor_tensor(kpe[:, 0:16], rp[:, 0, 0], rp[:, 1, 0], op=ALU.subtract), r=[rp], w=[kpe])
                    V(lambda e: e.tensor_tensor(kpe[:, 16:32], rp[:, 2, 0], rp[:, 3, 0], op=ALU.add), r=[rp], w=[kpe])
                    V(lambda e: e.tensor_copy(Kt[:, :, 64:96], bcast(kpe[:], [128, 8, 32], 1)), r=[kpe], w=[Kt])
                    DMA("sp", Vs[b, t0:t0 + 128, :], Vt[:].rearrange("p h f -> p (h f)"), r=[Vt], w=[bVs[b]])
                    for (src, stg, pa) in ((Qt, qs, 4), (Kt, ks, 6)):
                        for h in range(8):
                            pp = PS[pa + h // 4]
                            PE(lambda e: e.transpose(pp[0:96, (h % 4) * 128:(h % 4 + 1) * 128], src[:, h, :], ident[:]), r=[src, ident], w=[pp], acc=True)
                        A(lambda e: e.copy(stg[:, 0:4, tl * 128:(tl + 1) * 128], PS[pa][0:96, :].rearrange("p (k t) -> p k t", k=4)), r=[PS[pa]], w=[stg])
                        V(lambda e: e.tensor_copy(stg[:, 4:8, tl * 128:(tl + 1) * 128], PS[pa + 1][0:96, :].rearrange("p (k t) -> p k t", k=4)), r=[PS[pa + 1]], w=[stg])
                    if tl == 3:
                        DMA("sp", QTs[b, :, :, g4 * 512:(g4 + 1) * 512].rearrange("h f s -> f h s"), qs[:], r=[qs], w=[bQK[b][g4]])
                        DMA("sp", KTs[b, :, :, g4 * 512:(g4 + 1) * 512].rearrange("h f s -> f h s"), ks[:], r=[ks], w=[bQK[b][g4]])

        with ExitStack() as e2:
            kb.barrier()
            g_att = load_bc(e2, "g_att", attn_out_g, 512)
            Vaug = T(nc, e2, "Vaug", [128, NT, 8 * 65], BF16)
            Ab = T(nc, e2, "Ab", [128, NT, 512], BF16, nb=NG)
            QTh = [T(nc, e2, "QTh%d" % i, [96, S], BF16) for i in range(2)]
            KTh = [T(nc, e2, "KTh%d" % i, [96, S], BF16) for i in range(2)]
            PT = [T(nc, e2, "PT%d" % i, [128, 512], BF16) for i in range(5)]
            SR = [PS[0], PS[1], PS[2], PS[6], PS[7]]
            OT = T(nc, e2, "OT", [65, 512], F32)
            o_n = T(nc, e2, "o_n", [128, 4, 64], F32)
            o_sq = T(nc, e2, "o_sq", [128, 4, 64], F32)
            st2 = T(nc, e2, "st2", [128, 16], F32)
            for b in range(NB):
                DMA("sp", Vaug[:], Vs[b].rearrange("(nt p) c -> p nt c", p=128), r=[bVs[b]], w=[Vaug])
                for h in range(8):
                    qh = QTh[h % 2]; kh = KTh[h % 2]
                    DMA("sp", qh[:], QTs[b, h], r=bQK[b], w=[qh])
                    DMA("sp", kh[:], KTs[b, h], r=bQK[b], w=[kh])
                    for g in range(NG):
                        pO = PS[3 + g % 2]
                        cnt = [0]

                        def qk(kc):
                            ps = SR[kc % 5]
                            PE(lambda e: e.matmul(ps[:, :], lhsT=kh[:, kc * 128:(kc + 1) * 128], rhs=qh[:, g * 512:(g + 1) * 512], start=True, stop=True), r=[kh, qh], w=[ps])
                        for kc0 in range(min(4, NT)):
                            qk(kc0)
                        for kc in range(NT):
                            if kc + 4 < NT:
                                qk(kc + 4)
                            ps = SR[kc % 5]; pt = PT[kc % 5]
                            A(lambda e: e.activation(pt[:], ps[:, :], AF.Exp), r=[ps], w=[pt])
                            PE(lambda e: e.matmul(pO[0:65, :], lhsT=Vaug[:, kc, h * 65:(h + 1) * 65], rhs=pt[:], start=(kc == 0), stop=(kc == NT - 1)), r=[Vaug, pt], w=[pO], acc=True)
                        V(lambda e: e.tensor_copy(OT[:], pO[0:65, :]), r=[pO], w=[OT])
                        for j in range(4):
                            PE(lambda e: e.transpose(PS[5][:, j * 65:(j + 1) * 65], OT[0:65, j * 128:(j + 1) * 128], ident[0:65, 0:65]), r=[OT, ident], w=[PS[5]], acc=True)
                        p5 = PS[5][:, 0:260].rearrange("p (j f) -> p j f", j=4)
                        V(lambda e: e.reciprocal(st2[:, 0:4], p5[:, :, 64]), r=[PS[5]], w=[st2])
                        V(lambda e: e.tensor_tensor(o_n[:], p5[:, :, 0:64], bcast(st2[:, 0:4], [128, 4, 64], 2), op=ALU.mult), r=[PS[5], st2], w=[o_n])
                        G(lambda e: e.tensor_tensor(o_sq[:], o_n[:], o_n[:], op=ALU.mult), r=[o_n], w=[o_sq])
                        V(lambda e: e.tensor_reduce(out=st2[:, 4:8], in_=o_sq[:], axis=AX.X, op=ALU.add), r=[o_sq], w=[st2])
                        V(lambda e: e.tensor_scalar(st2[:, 4:8], st2[:, 4:8], 1.0 / 64, 1e-6, op0=ALU.mult, op1=ALU.add), r=[st2], w=[st2])
                        A(lambda e: e.activation(st2[:, 4:8], st2[:, 4:8], AF.Ln), r=[st2], w=[st2])
                        A(lambda e: e.activation(st2[:, 4:8], st2[:, 4:8], AF.Exp, scale=-0.5), r=[st2], w=[st2])
                        V(lambda e: e.tensor_tensor(o_n[:], o_n[:], bcast(st2[:, 4:8], [128, 4, 64], 2), op=ALU.mult), r=[o_n, st2], w=[o_n])
                        G(lambda e: e.tensor_tensor(Ab[:, g * 4:(g + 1) * 4, h * 64:(h + 1) * 64], o_n[:], bcast(g_att[:, h * 64:(h + 1) * 64], [128, 4, 64], 1), op=ALU.mult), r=[o_n, g_att], w=[Ab.bs[g]])
                DMA("sp", As[b].rearrange("(nt p) c -> p nt c", p=128), Ab[:], r=Ab.bs, w=[bAs[b]])

        with ExitStack() as e3:
            kb.barrier()
            absd = load_bc(e3, "absd", c_absd, 512)
            tneg = T(nc, e3, "tneg", [128, NT], F32)
            DMA("sp", tneg[:], c_tneg, w=[tneg])
            hyb = load_bc(e3, "hyb", hy_bias, 1024)
            g_hy = load_bc(e3, "g_hy", hy_out_g, 512)
            alt = T(nc, e3, "alt", [128, 2], BF16)
            altf = T(nc, e3, "altf", [128, 1], F32)
            G(lambda e: e.iota(iot[:, 0:1], pattern=[[0, 1]], base=0, channel_multiplier=1), r=[iot], w=[iot])
            V(lambda e: e.tensor_scalar(iot[:, 0:1], iot[:, 0:1], 1, None, op0=ALU.bitwise_and), r=[iot], w=[iot])
            V(lambda e: e.tensor_copy(altf[:], iot[:, 0:1]), r=[iot], w=[altf])
            V(lambda e: e.tensor_scalar(alt[:, 0:1], altf[:], -2.0, 1.0, op0=ALU.mult, op1=ALU.add), r=[altf], w=[alt])
            with ExitStack() as e3a:
                zT = T(nc, e3a, "zT", [33, S], F32)
                DMA("sp", zT[:], c_zT, w=[zT])
                w1 = T(nc, e3a, "fw1", [33, 64], F32); DMA("sp", w1[:], fw1, w=[w1])
                w2 = T(nc, e3a, "fw2", [64, 64], F32); DMA("sp", w2[:], fw2, w=[w2])
                w3 = T(nc, e3a, "fw3", [64, 2048], F32); DMA("sp", w3[:], fw3, w=[w3])
                fp = T(nc, e3a, "fp", [64, 4], F32)
                for ci, src in enumerate((fb1, ffr1, fb2, ffr2)):
                    DMA("sp", fp[:, ci:ci + 1], src, w=[fp])
                h1T = T(nc, e3a, "h1T", [64, S], F32)
                h2T = T(nc, e3a, "h2T", [64, S], F32)
                ya = T(nc, e3a, "ya", [64, 512], F32); yb = T(nc, e3a, "yb", [64, 512], F32); yc = T(nc, e3a, "yc", [64, 512], F32)

                def sin_layer(dst, wT, K, srcT, bc, fc):
                    for c in range(S // 512):
                        PE(lambda e: e.matmul(PS[0][0:64, :], lhsT=wT[0:K, 0:64], rhs=srcT[0:K, c * 512:(c + 1) * 512], start=True, stop=True), r=[wT, srcT], w=[PS[0]])
                        V(lambda e: e.tensor_scalar(ya[:], PS[0][0:64, :], fp[:, bc:bc + 1], fp[:, fc:fc + 1], op0=ALU.add, op1=ALU.mult), r=[PS[0], fp], w=[ya])
                        A(lambda e: e.activation(yb[:], ya[:], AF.Abs), r=[ya], w=[yb])
                        A(lambda e: e.activation(yc[:], ya[:], AF.Sin, scale=0.5), r=[ya], w=[yc])
                        A(lambda e: e.activation(yb[:], yb[:], AF.Sin, bias=cst[0:64, 2:3], scale=-0.5), r=[yb, cst], w=[yb])
                        V(lambda e: e.scalar_tensor_tensor(out=dst[:, c * 512:(c + 1) * 512], in0=yc[:], scalar=2.0, in1=yb[:], op0=ALU.mult, op1=ALU.mult), r=[yc, yb], w=[dst])
                sin_layer(h1T, w1, 33, zT, 0, 1)
                sin_layer(h2T, w2, 64, h1T, 2, 3)
                dec = [T(nc, e3a, "dec%d" % i, [128, 512], F32) for i in range(2)]
                hd_ = [T(nc, e3a, "hd%d" % i, [128, 512], F32) for i in range(2)]
                hab = [T(nc, e3a, "hab%d" % i, [128, 512], F32) for i in range(2)]
                invs = T(nc, e3a, "invs", [1, 2048], F32)
                invbc = T(nc, e3a, "invbc", [128, 2048], F32)
                hn = [T(nc, e3a, "hn%d" % i, [128, 512], F32) for i in range(2)]
                hsd = [T(nc, e3a, "hsd%d" % i, [128, 2, 512], BF16) for i in range(2)]

                def mkdec(tc):
                    d = dec[tc % 2]
                    A(lambda e: e.activation(d[:], absd[:], AF.Exp, scale=tneg[:, tc:tc + 1]), r=[absd, tneg], w=[d])
                    return d
                for tc in range(NT):
                    d = mkdec(tc)
                    for cg in range(4):
                        pp = PS[cg % 2]; hh = hd_[cg % 2]; ha = hab[cg % 2]
                        PE(lambda e: e.matmul(pp[:, :], lhsT=h2T[0:64, tc * 128:(tc + 1) * 128], rhs=w3[0:64, cg * 512:(cg + 1) * 512], start=True, stop=True), r=[h2T, w3], w=[pp])
                        V(lambda e: e.tensor_tensor(hh[:], pp[:, :], d[:], op=ALU.mult), r=[pp, d], w=[hh])
                        A(lambda e: e.activation(ha[:], hh[:], AF.Abs), r=[hh], w=[ha])
                        PE(lambda e: e.matmul(PS[4 + cg][0:1, :], lhsT=ones_f[:, 0:1], rhs=ha[:], start=(tc == 0), stop=(tc == NT - 1)), r=[ones_f, ha], w=[PS[4 + cg]], acc=True)
                for cg in range(4):
                    V(lambda e: e.tensor_scalar(invs[:, cg * 512:(cg + 1) * 512], PS[4 + cg][0:1, :], 1e-6, None, op0=ALU.add), r=[PS[4 + cg]], w=[invs])
                V(lambda e: e.reciprocal(invs[:], invs[:]), r=[invs], w=[invs])
                for cg in range(4):
                    PE(lambda e: e.matmul(PS[cg % 2][:, :], lhsT=ones_f[0:1, :], rhs=invs[0:1, cg * 512:(cg + 1) * 512], start=True, stop=True), r=[ones_f, invs], w=[PS[cg % 2]])
                    V(lambda e: e.tensor_copy(invbc[:, cg * 512:(cg + 1) * 512], PS[cg % 2][:, :]), r=[PS[cg % 2]], w=[invbc])
                for tc in range(NT):
                    d = mkdec(tc)
                    for o in range(2):
                        for dr in range(2):
                            cg = o * 2 + dr
                            pp = PS[cg % 2]
                            PE(lambda e: e.matmul(pp[:, :], lhsT=h2T[0:64, tc * 128:(tc + 1) * 128], rhs=w3[0:64, cg * 512:(cg + 1) * 512], start=True, stop=True), r=[h2T, w3], w=[pp])
                            V(lambda e: e.tensor_tensor(hn[dr][:], pp[:, :], d[:], op=ALU.mult), r=[pp, d], w=[hn[dr]])
                            G(lambda e: e.tensor_tensor(hn[dr][:], hn[dr][:], invbc[:, cg * 512:(cg + 1) * 512], op=ALU.mult), r=[hn[dr], invbc], w=[hn[dr]])
                        so = hsd[o]
                        G(lambda e: e.tensor_tensor(so[:, 0, :], hn[0][:], hn[1][:], op=ALU.add), r=[hn[0], hn[1]], w=[so])
                        V(lambda e: e.tensor_tensor(so[:, 1, :], hn[0][:], hn[1][:], op=ALU.subtract), r=[hn[0], hn[1]], w=[so])
                        DMA("sp", HSs[o, :, tc * 128:(tc + 1) * 128, :].rearrange("a p c -> p a c"), so[:], r=[so], w=[bHSs[o]])
            kb.barrier()
            tab = T(nc, e3, "tab", [128, 3, N2, 128], BF16, nb=3)
            tmp = [T(nc, e3, "tmp%d" % i, [128, 512], F32) for i in range(6)]
            tmpA = tmp[0:4]
            tmpB = [T(nc, e3, "tmpb%d" % i, [128, 512], F32) for i in range(4)]

            def load_fwd(j):
                u = j % 3
                kb.dma("sp", lambda e: e.dma_start(out=tab[:, u, 0:NT, :], in_=c_fwd[j]), writes=[tab.bs[u]])
                kb.dma("sp", lambda e: e.dma_start(out=tab[:, u, NT:N2, :], in_=c_fwd[NT + j]), writes=[tab.bs[u]])

            with ExitStack() as e3b:
                kb.barrier()
                HS = T(nc, e3b, "HS", [128, 2, NT, 512], BF16)
                hft = [T(nc, e3b, "hft%d" % i, [128, 2, 512], BF16) for i in range(2)]
                for o in range(2):
                    DMA("sp", HS[:, 0], HSs[o, 0].rearrange("(nt p) c -> p nt c", p=128), r=[bHSs[o]], w=[HS])
                    DMA("sp", HS[:, 1], HSs[o, 1].rearrange("(nt p) c -> p nt c", p=128), r=[bHSs[o]], w=[HS])
                    for j in range(NT):
                        s = j % 2
                        u = j % 3
                        if j == 0:
                            load_fwd(0)
                        if j + 1 < NT:
                            load_fwd(j + 1)
                        hf = hft[j % 2]
                        for tc in range(NT):
                            PE(lambda e: e.matmul(PS[2 * s][:, :], lhsT=tab[:, u, tc, :], rhs=HS[:, 0, tc, :], start=(tc == 0), stop=(tc == NT - 1)), r=[tab.bs[u], HS], w=[PS[2 * s]], acc=True)
                        for tc in range(NT):
                            PE(lambda e: e.matmul(PS[2 * s + 1][:, :], lhsT=tab[:, u, NT + tc, :], rhs=HS[:, 1, tc, :], start=(tc == 0), stop=(tc == NT - 1)), r=[tab.bs[u], HS], w=[PS[2 * s + 1]], acc=True)
                        A(lambda e: e.copy(hf[:, 0, :], PS[2 * s][:, :]), r=[PS[2 * s]], w=[hf])
                        V(lambda e: e.tensor_copy(hf[:, 1, :], PS[2 * s + 1][:, :]), r=[PS[2 * s + 1]], w=[hf])
                        if j == 0:
                            for tc in range(NT):
                                PE(lambda e: e.matmul(PS[6][0:1, :], lhsT=alt[:, 0:1], rhs=HS[:, 0, tc, :], start=(tc == 0), stop=(tc == NT - 1)), r=[alt, HS], w=[PS[6]], acc=True)
                            V(lambda e: e.tensor_copy(hf[0:1, 1, :], PS[6][0:1, :]), r=[PS[6]], w=[hf])
                        DMA("sp", Hfs[o, j], hf[:], r=[hf], w=[bHfs[o][j]])

            with ExitStack() as e3c:
                kb.barrier()
                zb = T(nc, e3c, "zb", [128, NT, 512], BF16)
                Yf = T(nc, e3c, "Yf", [128, N2, 512], BF16)
                hft = [T(nc, e3c, "hfu%d" % i, [128, 2, 512], BF16) for i in range(2)]
                gt = [T(nc, e3c, "gt%d" % i, [128, 512], BF16) for i in range(2)]
                yt = [T(nc, e3c, "yt%d" % i, [128, 512], BF16) for i in range(2)]
                st3 = T(nc, e3c, "st3", [128, 16], F32)
                for b in range(NB):
                    DMA("sp", zb[:], Us[b, :, 0:512].rearrange("(nt p) c -> p nt c", p=128), r=bUs[b], w=[zb])
                    for o in range(2):
                        for j in range(NT):
                            s = j % 2
                            u = j % 3
                            load_fwd(j)
                            hf = hft[j % 2]
                            tmq = tmpA if s == 0 else tmpB
                            DMA("sp", hf[:], Hfs[o, j], r=[bHfs[o][j]], w=[hf])
                            for tc in range(NT):
                                PE(lambda e: e.matmul(PS[2 * s][:, :], lhsT=tab[:, u, tc, :], rhs=zb[:, tc, :], start=(tc == 0), stop=(tc == NT - 1)), r=[tab.bs[u], zb], w=[PS[2 * s]], acc=True)
                            for tc in range(NT):
                                PE(lambda e: e.matmul(PS[2 * s + 1][:, :], lhsT=tab[:, u, NT + tc, :], rhs=zb[:, tc, :], start=(tc == 0), stop=(tc == NT - 1)), r=[tab.bs[u], zb], w=[PS[2 * s + 1]], acc=True)
                            V(lambda e: e.tensor_tensor(tmq[0][:], PS[2 * s + 0][:, :], hf[:, 0, :], op=ALU.mult), r=[PS[2 * s + 0], hf], w=[tmq[0]])
                            V(lambda e: e.tensor_tensor(tmq[1][:], PS[2 * s + 1][:, :], hf[:, 1, :], op=ALU.mult), r=[PS[2 * s + 1], hf], w=[tmq[1]])
                            V(lambda e: e.tensor_tensor(tmq[2][:], PS[2 * s + 0][:, :], hf[:, 1, :], op=ALU.mult), r=[PS[2 * s + 0], hf], w=[tmq[2]])
                            V(lambda e: e.tensor_tensor(tmq[3][:], PS[2 * s + 1][:, :], hf[:, 0, :], op=ALU.mult), r=[PS[2 * s + 1], hf], w=[tmq[3]])
                            G(lambda e: e.tensor_tensor(Yf[:, j, :], tmq[0][:], tmq[1][:], op=ALU.subtract), r=[tmq[0], tmq[1]], w=[Yf])
                            G(lambda e: e.tensor_tensor(Yf[:, NT + j, :], tmq[2][:], tmq[3][:], op=ALU.add), r=[tmq[2], tmq[3]], w=[Yf])
                            if j == 0:
                                A(lambda e: e.copy(Yf[0:1, 0, :], tmq[0][0:1, :]), r=[tmq[0]], w=[Yf])
                                A(lambda e: e.copy(Yf[0:1, NT, :], tmq[1][0:1, :]), r=[tmq[1]], w=[Yf])
                        def ld_inv(tc_):
                            kb.dma("sp", lambda e: e.dma_start(out=tab[:, tc_ % 3], in_=c_inv[tc_]), writes=[tab.bs[tc_ % 3]])
                            DMA("sp", gt[tc_ % 2][:], Us[b, tc_ * 128:(tc_ + 1) * 128, (1 + o) * 512:(2 + o) * 512], r=[bUs[b][tc_]], w=[gt[tc_ % 2]])
                        for tc in range(NT):
                            s = tc % 2
                            u = tc % 3
                            gg = gt[tc % 2]
                            if tc == 0:
                                ld_inv(0)
                            if tc + 1 < NT:
                                ld_inv(tc + 1)
                            for kc in range(N2):
                                PE(lambda e: e.matmul(PS[4 + u][:, :], lhsT=tab[:, u, kc, :], rhs=Yf[:, kc, :], start=(kc == 0), stop=(kc == N2 - 1)), r=[tab.bs[u], Yf], w=[PS[4 + u]], acc=True)
                            G(lambda e: e.tensor_tensor(tmp[4][:], zb[:, tc, :], hyb[:, o * 512:(o + 1) * 512], op=ALU.mult), r=[zb, hyb], w=[tmp[4]])
                            V(lambda e: e.tensor_tensor(tmp[4][:], PS[4 + u][:, :], tmp[4][:], op=ALU.add), r=[PS[4 + u], tmp[4]], w=[tmp[4]])
                            if o == 0:
                                G(lambda e: e.tensor_tensor(zb[:, tc, :], tmp[4][:], gg[:], op=ALU.mult), r=[tmp[4], gg], w=[zb])
                            else:
                                G(lambda e: e.tensor_tensor(tmp[5][:], tmp[4][:], gg[:], op=ALU.mult), r=[tmp[4], gg], w=[tmp[5]])
                                A(lambda e: e.activation(tmp[4][:], tmp[5][:], AF.Square), r=[tmp[5]], w=[tmp[4]])
                                V(lambda e: e.tensor_reduce(out=st3[:, 0:8], in_=tmp[4][:].rearrange("p (g f) -> p g f", g=8), axis=AX.X, op=ALU.add), r=[tmp[4]], w=[st3])
                                rstd_from_ssq(st3, slice(0, 8), slice(0, 8), 1.0 / 64, 1)
                                V(lambda e: e.tensor_tensor(tmp[5][:].rearrange("p (g f) -> p g f", g=8), tmp[5][:].rearrange("p (g f) -> p g f", g=8), bcast(st3[:, 0:8], [128, 8, 64], 2), op=ALU.mult), r=[tmp[5], st3], w=[tmp[5]])
                                yy = yt[tc % 2]
                                G(lambda e: e.tensor_tensor(yy[:], tmp[5][:], g_hy[:], op=ALU.mult), r=[tmp[5], g_hy], w=[yy])
                                DMA("sp", Ys[b, tc * 128:(tc + 1) * 128, :], yy[:], r=[yy], w=[bYs[b][tc]])

        with ExitStack() as e4:
            kb.barrier()
            w_o_b = T(nc, e4, "w_o_b", [128, 8, D], BF16)
            w_q_b = T(nc, e4, "w_q_b", [128, 8, D], BF16)
            kb.dma("pool", lambda e: e.dma_start(out=w_o_b[:], in_=w_o.rearrange("(k p) n -> p k n", p=128)), writes=[w_o_b.b])
            kb.dma("pool", lambda e: e.dma_start(out=w_q_b[:], in_=peer_wq.rearrange("(k p) n -> p k n", p=128)), writes=[w_q_b.b])
            g_mix = load_bc(e4, "g_mix", ln_mix_g, D); b_mix = load_bc(e4, "b_mix", ln_mix_b, D)
            keysBD = T(nc, e4, "keysBD", [128, 8, 256], BF16)
            V(lambda e: e.memset(keysBD[:], 0.0), w=[keysBD])
            knat = T(nc, e4, "knat", [128, 8, 2, 64], F32)
            DMA("sp", knat[:], peer_keys.rearrange("h p n d -> n h p d"), w=[knat])
            for h in range(8):
                PE(lambda e: e.transpose(PS[h % 2][:, 0:128], knat[:, h].rearrange("n p d -> n (p d)"), ident[:]), r=[knat, ident], w=[PS[h % 2]])
                A(lambda e: e.copy(keysBD[0:64, h, 0:128], PS[h % 2][0:64, 0:128]), r=[PS[h % 2]], w=[keysBD])
                V(lambda e: e.tensor_copy(keysBD[64:128, h, 128:256], PS[h % 2][64:128, 0:128]), r=[PS[h % 2]], w=[keysBD])
            ioc = T(nc, e4, "ioc", [128, 16, 128], I32)
            G(lambda e: e.iota(ioc[:], pattern=[[0, 16], [1, 128]], base=0, channel_multiplier=0), w=[ioc])
            cat_2 = [T(nc, e4, "cat_%d" % q_, [128, D], F32) for q_ in range(2)]
            catb_2 = [T(nc, e4, "catb_%d" % q_, [128, D], BF16) for q_ in range(2)]
            catT_2 = [T(nc, e4, "catT_%d" % q_, [128, 8, 128], BF16) for q_ in range(2)]
            hres_2 = [T(nc, e4, "hres_%d" % q_, [128, D], F32) for q_ in range(2)]
            r1_2 = [T(nc, e4, "r1_%d" % q_, [128, D], F32) for q_ in range(2)]
            h2_2 = [T(nc, e4, "h2_%d" % q_, [128, D], F32) for q_ in range(2)]
            h2T_2 = [T(nc, e4, "h2Tp_%d" % q_, [128, 8, 128], BF16) for q_ in range(2)]
            qpT_2 = [T(nc, e4, "qpT_%d" % q_, [128, 8, 128], BF16) for q_ in range(2)]
            sc_2 = [T(nc, e4, "sc_%d" % q_, [128, 16, 128], F32) for q_ in range(2)]
            sct_2 = [T(nc, e4, "sct_%d" % q_, [128, 16, 128], F32) for q_ in range(2)]
            sv_2 = [T(nc, e4, "sv_%d" % q_, [128, 16, 16], F32) for q_ in range(2)]
            si_2 = [T(nc, e4, "si_%d" % q_, [128, 16, 16], I32) for q_ in range(2)]
            cand_2 = [T(nc, e4, "cand_%d" % q_, [128, 8, 256], F32) for q_ in range(2)]
            eid_2 = [T(nc, e4, "eid_%d" % q_, [128, 8, 256], I32) for q_ in range(2)]
            best_2 = [T(nc, e4, "best_%d" % q_, [128, 8, 16], F32) for q_ in range(2)]
            eii_2 = [T(nc, e4, "eii_%d" % q_, [128, 128], I32) for q_ in range(2)]
            gat_2 = [T(nc, e4, "gat_%d" % q_, [128, 8, 16], F32) for q_ in range(2)]
            st4_2 = [T(nc, e4, "st4_%d" % q_, [128, 32], F32) for q_ in range(2)]
            eij_2 = [T(nc, e4, "eij_%d" % q_, [128, 2, 128], I32) for q_ in range(2)]
            ijg_2 = [T(nc, e4, "ijg_%d" % q_, [128, 3, 128], F32) for q_ in range(2)]
            ijgT_2 = [T(nc, e4, "ijgT_%d" % q_, [128, 3, 128], F32) for q_ in range(2)]
            def ld4(b_, i_):
                q_ = (b_ * NT + i_) % 2
                DMA("sp", catb_2[q_][:, 0:512], As[b_, i_ * 128:(i_ + 1) * 128, :], r=[bAs[b_]], w=[catb_2[q_]])
                DMA("sp", catb_2[q_][:, 512:1024], Ys[b_, i_ * 128:(i_ + 1) * 128, :], r=[bYs[b_][i_]], w=[catb_2[q_]])
                DMA("sp", hres_2[q_][:], Hs[b_, i_ * 128:(i_ + 1) * 128, :], r=[bHs[b_][i_]], w=[hres_2[q_]])
            def tile4(b, i):
                t0 = i * 128
                tp_ = (b * NT + i) % 2
                cat = cat_2[tp_]
                catb = catb_2[tp_]
                catT = catT_2[tp_]
                hres = hres_2[tp_]
                r1 = r1_2[tp_]
                h2 = h2_2[tp_]
                h2T = h2T_2[tp_]
                qpT = qpT_2[tp_]
                sc = sc_2[tp_]
                sct = sct_2[tp_]
                sv = sv_2[tp_]
                si = si_2[tp_]
                cand = cand_2[tp_]
                eid = eid_2[tp_]
                best = best_2[tp_]
                eii = eii_2[tp_]
                gat = gat_2[tp_]
                st4 = st4_2[tp_]
                eij = eij_2[tp_]
                ijg = ijg_2[tp_]
                ijgT = ijgT_2[tp_]
                ld4(b, i)
                A(lambda e: e.copy(cat[:], catb[:]), r=[catb], w=[cat])
                for k in range(8):
                    pp = PS[k // 4]
                    PE(lambda e: e.transpose(pp[:, (k % 4) * 128:(k % 4 + 1) * 128], cat[:, k * 128:(k + 1) * 128], ident[:]), r=[cat, ident], w=[pp], acc=True)
                A(lambda e: e.copy(catT[:, 0:4, :], PS[0][:].rearrange("p (k t) -> p k t", k=4)), r=[PS[0]], w=[catT])
                V(lambda e: e.tensor_copy(catT[:, 4:8, :], PS[1][:].rearrange("p (k t) -> p k t", k=4)), r=[PS[1]], w=[catT])
                yield
                for hf_ in range(2):
                    for k in range(8):
                        PE(lambda e: e.matmul(PS[2 + hf_][:, :], lhsT=catT[:, k, :], rhs=w_o_b[:, k, hf_ * 512:(hf_ + 1) * 512], start=(k == 0), stop=(k == 7)), r=[catT, w_o_b], w=[PS[2 + hf_]], acc=True)
                alpha = 2.0 ** 0.25
                for hf_ in range(2):
                    V(lambda e: e.scalar_tensor_tensor(out=r1[:, hf_ * 512:(hf_ + 1) * 512], in0=hres[:, hf_ * 512:(hf_ + 1) * 512], scalar=alpha, in1=PS[2 + hf_][:, :], op0=ALU.mult, op1=ALU.add), r=[hres, PS[2 + hf_]], w=[r1])
                layer_norm(r1[:], [r1], h2, g_mix, b_mix, st4, h2)
                yield
                for k in range(8):
                    pp = PS[k // 4]
                    PE(lambda e: e.transpose(pp[:, (k % 4) * 128:(k % 4 + 1) * 128], h2[:, k * 128:(k + 1) * 128], ident[:]), r=[h2, ident], w=[pp], acc=True)
                A(lambda e: e.copy(h2T[:, 0:4, :], PS[0][:].rearrange("p (k t) -> p k t", k=4)), r=[PS[0]], w=[h2T])
                V(lambda e: e.tensor_copy(h2T[:, 4:8, :], PS[1][:].rearrange("p (k t) -> p k t", k=4)), r=[PS[1]], w=[h2T])
                yield
                for h in range(8):
                    pp = PS[2 + h // 4]
                    for k in range(8):
                        PE(lambda e: e.matmul(pp[:, (h % 4) * 128:(h % 4 + 1) * 128], lhsT=w_q_b[:, k, h * 128:(h + 1) * 128], rhs=h2T[:, k, :], start=(k == 0), stop=(k == 7)), r=[w_q_b, h2T], w=[pp], acc=True)
                A(lambda e: e.copy(qpT[:, 0:4, :], PS[2][:].rearrange("p (k t) -> p k t", k=4)), r=[PS[2]], w=[qpT])
                V(lambda e: e.tensor_copy(qpT[:, 4:8, :], PS[3][:].rearrange("p (k t) -> p k t", k=4)), r=[PS[3]], w=[qpT])
                yield
                for h in range(8):
                    pp = PS[4 + h // 2]
                    PE(lambda e: e.matmul(pp[:, (h % 2) * 256:(h % 2 + 1) * 256], lhsT=qpT[:, h, :], rhs=keysBD[:, h, :], start=True, stop=True), r=[qpT, keysBD], w=[pp], acc=True)
                for q4 in range(4):
                    (A if q4 % 2 == 0 else V)(lambda e: (e.copy if q4 % 2 == 0 else e.tensor_copy)(sc[:, q4 * 4:(q4 + 1) * 4, :], PS[4 + q4][:].rearrange("p (g n) -> p g n", g=4)), r=[PS[4 + q4]], w=[sc])
                V(lambda e: e.tensor_scalar(sc[:].bitcast(I32), sc[:].bitcast(I32), -128, None, op0=ALU.bitwise_and), r=[sc], w=[sc])
                V(lambda e: e.tensor_tensor(sc[:].bitcast(I32), sc[:].bitcast(I32), ioc[:], op=ALU.bitwise_or), r=[sc, ioc], w=[sc])
                yield
                scp = sc[:]
                for g in range(16):
                    V(lambda e: e.max(out=sv[:, g, 0:8], in_=scp[:, g, :]), r=[sc], w=[sv])
                    V(lambda e: e.match_replace(out=sct[:, g, :], in_to_replace=sv[:, g, 0:8], in_values=scp[:, g, :], imm_value=NEG), r=[sc, sv], w=[sct])
                    V(lambda e: e.max(out=sv[:, g, 8:16], in_=sct[:, g, :]), r=[sct], w=[sv])
                    if g % 4 == 3:
                        yield
                V(lambda e: e.tensor_scalar(si[:], sv[:].bitcast(I32), 127, None, op0=ALU.bitwise_and), r=[sv], w=[si])
                sv4 = sv[:].rearrange("p (h q) k -> p h q k", q=2)
                si4 = si[:].rearrange("p (h q) k -> p h q k", q=2)
                c4 = cand[:].rearrange("p h (i j) -> p h i j", i=16)
                e4v = eid[:].rearrange("p h (i j) -> p h i j", i=16)
                V(lambda e: e.tensor_tensor(c4, bcast(sv4[:, :, 0, :], [128, 8, 16, 16], 3), bcast(sv4[:, :, 1, :], [128, 8, 16, 16], 2), op=ALU.add), r=[sv], w=[cand])
                V(lambda e: e.tensor_scalar(si4[:, :, 0, :], si4[:, :, 0, :], 7, None, op0=ALU.logical_shift_left), r=[si], w=[si])
                V(lambda e: e.tensor_tensor(e4v, bcast(si4[:, :, 0, :], [128, 8, 16, 16], 3), bcast(si4[:, :, 1, :], [128, 8, 16, 16], 2), op=ALU.bitwise_or), r=[si], w=[eid])
                V(lambda e: e.tensor_scalar(cand[:].bitcast(I32), cand[:].bitcast(I32), -16384, None, op0=ALU.bitwise_and), r=[cand], w=[cand])
                V(lambda e: e.tensor_tensor(cand[:].bitcast(I32), cand[:].bitcast(I32), eid[:], op=ALU.bitwise_or), r=[cand, eid], w=[cand])
                yield
                cp = cand[:]
                candt = sct[:].rearrange("p (h a) n -> p h (a n)", a=2)
                for h in range(8):
                    V(lambda e: e.max(out=best[:, h, 0:8], in_=cp[:, h, :]), r=[cand], w=[best])
                    V(lambda e: e.match_replace(out=candt[:, h, :], in_to_replace=best[:, h, 0:8], in_values=cp[:, h, :], imm_value=NEG), r=[cand, best], w=[sct])
                    V(lambda e: e.max(out=best[:, h, 8:16], in_=candt[:, h, :]), r=[sct], w=[best])
                    if h % 4 == 3:
                        yield
                V(lambda e: e.tensor_tensor(gat[:], best[:], bcast(best[:, :, 0], [128, 8, 16], 2), op=ALU.subtract), r=[best], w=[gat])
                A(lambda e: e.activation(gat[:], gat[:], AF.Exp), r=[gat], w=[gat])
                V(lambda e: e.tensor_reduce(out=st4[:, 8:16], in_=gat[:], axis=AX.X, op=ALU.add), r=[gat], w=[st4])
                V(lambda e: e.reciprocal(st4[:, 8:16], st4[:, 8:16]), r=[st4], w=[st4])
                V(lambda e: e.tensor_tensor(gat[:], gat[:], bcast(st4[:, 8:16], [128, 8, 16], 2), op=ALU.mult), r=[gat, st4], w=[gat])
                yield
                V(lambda e: e.tensor_scalar(eii[:], best[:].rearrange("p h k -> p (h k)").bitcast(I32), 16383, None, op0=ALU.bitwise_and), r=[best], w=[eii])
                V(lambda e: e.tensor_scalar(eij[:, 0, :], eii[:], 7, None, op0=ALU.arith_shift_right), r=[eii], w=[eij])
                V(lambda e: e.tensor_scalar(eij[:, 1, :], eii[:], 127, None, op0=ALU.bitwise_and), r=[eii], w=[eij])
                V(lambda e: e.tensor_copy(ijg[:, 0:2, :], eij[:]), r=[eij], w=[ijg])
                G(lambda e: e.tensor_copy(ijg[:, 2, :], gat[:].rearrange("p h k -> p (h k)")), r=[gat], w=[ijg])
                for c3 in range(3):
                    PE(lambda e: e.transpose(PS[6][:, c3 * 128:(c3 + 1) * 128], ijg[:, c3, :], ident[:]), r=[ijg, ident], w=[PS[6]], acc=True)
                A(lambda e: e.copy(ijgT[:], PS[6][:, 0:384].rearrange("p (c t) -> p c t", c=3)), r=[PS[6]], w=[ijgT])
                DMA("sp", IJG[b, i], ijgT[:], r=[ijgT], w=[bP5[b][i]])
                DMA("sp", H2s[b, t0:t0 + 128, :], h2[:], r=[h2], w=[bP5[b][i]])
                DMA("sp", H2Ts[b, i], h2T[:], r=[h2T], w=[bP5[b][i]])


            u_v = peer_u.rearrange("(i j) d -> j i d", j=128)
            v_v = peer_v.rearrange("(i j) d -> j i d", j=128)
            ur = [T(nc, e4, "ur%d" % i_, [128, D], F32) for i_ in range(2)]
            utl = [T(nc, e4, "utl%d" % i_, [128, 8, 128], BF16) for i_ in range(2)]
            vr = [T(nc, e4, "vr%d" % i_, [128, D], BF16) for i_ in range(2)]

            def prologue_gen():
                for j in range(128):
                    u_ = ur[j % 2]; ut_ = utl[j % 2]; v_ = vr[j % 2]
                    DMA("sp", u_[:], u_v[j], w=[u_])
                    for k in range(8):
                        pp = PS[k // 4]
                        PE(lambda e: e.transpose(pp[:, (k % 4) * 128:(k % 4 + 1) * 128], u_[:, k * 128:(k + 1) * 128], ident[:]), r=[u_, ident], w=[pp], acc=True)
                    A(lambda e: e.copy(ut_[:, 0:4, :], PS[0][:].rearrange("p (k t) -> p k t", k=4)), r=[PS[0]], w=[ut_])
                    A(lambda e: e.copy(ut_[:, 4:8, :], PS[1][:].rearrange("p (k t) -> p k t", k=4)), r=[PS[1]], w=[ut_])
                    DMA("sp", UTs[j], ut_[:], r=[ut_], w=[bUT])
                    kb.dma("pool", lambda e: e.dma_start(out=v_[:], in_=v_v[j]), writes=[v_.b])
                    DMA("sp", VBs[j], v_[:], r=[v_], w=[bVB])
                    yield

            pg = prologue_gen()
            gens = [tile4(b_, i_) for b_ in range(NB) for i_ in range(NT)]
            active = []
            gi = 0
            steps = 0
            while gi < len(gens) or active:
                if gi < len(gens) and (len(active) == 0 or (len(active) == 1 and steps >= 6)):
                    active.append(gens[gi]); gi += 1; steps = 0
                for g_ in list(active):
                    try:
                        next(g_)
                    except StopIteration:
                        active.remove(g_)
                steps += 1
                if steps % 2 == 0:
                    try:
                        next(pg)
                    except StopIteration:
                        pass
            for _ in pg:
                pass

        with ExitStack() as e5:
            kb.barrier()
            g_ffn = load_bc(e5, "g_ffn", ln_ffn_g, D); b_ffn = load_bc(e5, "b_ffn", ln_ffn_b, D)
            iof = T(nc, e5, "iof", [128, 128], F32)
            G(lambda e: e.iota(iot[:], pattern=[[1, 128]], base=0, channel_multiplier=0), r=[iot], w=[iot])
            V(lambda e: e.tensor_copy(iof[:], iot[:]), r=[iot], w=[iof])
            kb.barrier()
            Am = T(nc, e5, "Am", [128, 128, 128], BF16)
            Bm = T(nc, e5, "Bm", [128, 128, 128], BF16)
            ga = T(nc, e5, "ga", [128, 128, 256], BF16)
            uts = [T(nc, e5, "uts%d" % i_, [128, 2, 8, 128], BF16) for i_ in range(4)]
            vbs = [T(nc, e5, "vbs%d" % i_, [128, 2, D], BF16) for i_ in range(4)]
            gtm = [T(nc, e5, "gtm%d" % i_, [128, 2, 128], BF16) for i_ in range(2)]
            h2r = T(nc, e5, "h2r", [128, D], F32)
            h2Trs = [T(nc, e5, "h2Tr%d" % i_, [128, 8, 256], BF16) for i_ in range(2)]
            ijgrs = [T(nc, e5, "ijgr%d" % i_, [128, 2, 3, 128], F32) for i_ in range(2)]
            r5 = T(nc, e5, "r5", [128, D], F32)
            o5 = T(nc, e5, "o5", [128, D], F32)
            st5 = T(nc, e5, "st5", [128, 8], F32)
            alpha = 2.0 ** 0.25
            tiles = [(b_, i_) for b_ in range(NB) for i_ in range(NT)]
            NP = len(tiles) // 2
            iob = bcast(iof[:], [128, 128, 128], 1)

            def prep_loads(p_):
                for tt in range(2):
                    b_, i_ = tiles[2 * p_ + tt]
                    DMA("sp", ijgrs[p_ % 2][:, tt], IJG[b_, i_], r=[bP5[b_][i_]], w=[ijgrs[p_ % 2]])
                    DMA("sp", h2Trs[p_ % 2][:, :, tt * 128:(tt + 1) * 128], H2Ts[b_, i_], r=[bP5[b_][i_]], w=[h2Trs[p_ % 2]])

            def build_ab(p_, tt):
                ijgr = ijgrs[p_ % 2]
                for c8 in range(8):
                    ts_ = slice(c8 * 16, (c8 + 1) * 16)
                    iob16 = bcast(iof[:], [128, 16, 128], 1)
                    V(lambda e: e.tensor_tensor(Am[:, ts_, :], iob16, bcast(ijgr[:, tt, 0, ts_], [128, 16, 128], 2), op=ALU.is_equal), r=[iof, ijgr], w=[Am])
                    V(lambda e: e.tensor_tensor(Bm[:, ts_, :], iob16, bcast(ijgr[:, tt, 1, ts_], [128, 16, 128], 2), op=ALU.is_equal), r=[iof, ijgr], w=[Bm])
                    V(lambda e: e.tensor_tensor(Bm[:, ts_, :], Bm[:, ts_, :], bcast(ijgr[:, tt, 2, ts_], [128, 16, 128], 2), op=ALU.mult), r=[Bm, ijgr], w=[Bm])
                    yield

            def run_all(gen):
                for _ in gen:
                    pass

            def step(gen):
                if gen is not None:
                    try:
                        next(gen)
                    except StopIteration:
                        pass

            def g_phase(tt, first):
                for t4 in range(32):
                    pp = PS[2 + t4 % 2]
                    for tl in range(4):
                        t = t4 * 4 + tl
                        PE(lambda e: e.matmul(pp[:, tl * 128:(tl + 1) * 128], lhsT=Am[:, t, :], rhs=Bm[:, t, :], start=True, stop=True), r=[Am, Bm], w=[pp], acc=True)
                    gv = ga[:, :, tt * 128 + t4 * 4:tt * 128 + t4 * 4 + 4]
                    pv = pp[:].rearrange("p (t j) -> p j t", t=4)
                    if first:
                        if t4 % 2 == 0:
                            V(lambda e: e.tensor_copy(gv, pv), r=[pp], w=[ga])
                        else:
                            A(lambda e: e.copy(gv, pv), r=[pp], w=[ga])
                    else:
                        V(lambda e: e.tensor_tensor(gv, gv, pv, op=ALU.mult), r=[ga, pp], w=[ga])

            def ld_u(q_):
                DMA("sp", uts[q_ % 4][:], UTs[q_ * 2:(q_ + 1) * 2].rearrange("j p k i -> p j k i"), r=[bUT], w=[uts[q_ % 4]])

            def ld_v(q_):
                DMA("sp", vbs[q_ % 4][:], VBs[q_ * 2:(q_ + 1) * 2].rearrange("j p d -> p j d"), r=[bVB], w=[vbs[q_ % 4]])

            def epilogue(p_):
                for tt in range(2):
                    b, i = tiles[2 * p_ + tt]
                    t0 = i * 128
                    DMA("sp", h2r[:], H2s[b, t0:t0 + 128, :], r=[bP5[b][i]], w=[h2r])
                    for hf_ in range(2):
                        V(lambda e: e.scalar_tensor_tensor(out=r5[:, hf_ * 512:(hf_ + 1) * 512], in0=h2r[:, hf_ * 512:(hf_ + 1) * 512], scalar=alpha, in1=PS[4 + 2 * tt + hf_][:, :], op0=ALU.mult, op1=ALU.add), r=[h2r, PS[4 + 2 * tt + hf_]], w=[r5])
                        yield
                    for _ in layer_norm_gen(r5[:], [r5], o5, g_ffn, b_ffn, st5, o5):
                        yield
                    DMA("sp", out[b, t0:t0 + 128, :], o5[:], r=[o5])
                    yield

            prep_loads(0)
            run_all(build_ab(0, 0))
            g_phase(0, True)
            for p_ in range(NP):
                h2Tr = h2Trs[p_ % 2]
                gep = epilogue(p_ - 1) if p_ > 0 else None
                gen1 = build_ab(p_, 1)
                for q_ in range(3):
                    ld_u(q_)
                for c in range(64):
                    us = uts[c % 4]
                    if c + 3 < 64:
                        ld_u(c + 3)
                    pp = PS[c % 4]
                    for jj in range(2):
                        for k in range(8):
                            PE(lambda e: e.matmul(pp[:, jj * 256:(jj + 1) * 256], lhsT=us[:, jj, k, :], rhs=h2Tr[:, k, :], start=(k == 0), stop=(k == 7)), r=[us, h2Tr], w=[pp], acc=True)
                    pv = pp[:].rearrange("p (j t) -> p j t", j=2)
                    gt_ = gtm[c % 2]
                    A(lambda e: e.activation(gt_[:], pv[:, :, 0:128], AF.Gelu), r=[pp], w=[gt_])
                    A(lambda e: e.activation(ga[:, 2 * c:2 * c + 2, 128:256], pv[:, :, 128:256], AF.Gelu), r=[pp], w=[ga])
                    V(lambda e: e.tensor_tensor(ga[:, 2 * c:2 * c + 2, 0:128], ga[:, 2 * c:2 * c + 2, 0:128], gt_[:], op=ALU.mult), r=[ga, gt_], w=[ga])
                    if c % 6 == 5:
                        step(gen1)
                    else:
                        step(gep)
                run_all(gen1)
                if gep is not None:
                    run_all(gep)
                for q_ in range(3):
                    ld_v(q_)
                g_phase(1, False)
                gen0 = None
                if p_ + 1 < NP:
                    prep_loads(p_ + 1)
                    gen0 = build_ab(p_ + 1, 0)
                for c in range(64):
                    vs = vbs[c % 4]
                    if c + 3 < 64:
                        ld_v(c + 3)
                    for jj in range(2):
                        j = c * 2 + jj
                        for tt in range(2):
                            for hf_ in range(2):
                                PE(lambda e: e.matmul(PS[4 + 2 * tt + hf_][:, :], lhsT=ga[:, j, tt * 128:(tt + 1) * 128], rhs=vs[:, jj, hf_ * 512:(hf_ + 1) * 512], start=(j == 0), stop=(j == 127)), r=[ga, vs], w=[PS[4 + 2 * tt + hf_]], acc=True)
                    if c % 6 == 5:
                        step(gen0)
                if gen0 is not None:
                    run_all(gen0)
                    g_phase(0, True)
            run_all(epilogue(NP - 1))
        kb.drain()
    return nc


def _consts(S):
    NT = S // 128
    n = 2 * S
    t = np.arange(S)
    inv = (1.0 / (10000.0 ** (np.arange(0, 32, 2, dtype=np.float32) / np.float32(32)))).astype(np.float32)
    ang = t.astype(np.float32)[:, None] * inv[None, :]
    rope = np.concatenate([np.cos(ang), np.sin(ang)], 1).astype(np.float32)
    tl = np.linspace(0.0, 1.0, S, dtype=np.float32)[:, None]
    w = (np.float32(2.0 * math.pi) * np.arange(S, dtype=np.float32) / np.float32(S)).astype(np.float32)
    f = np.linspace(1e-4, 15, 16, dtype=np.float32)
    angz = w[:, None] * f[None, :]
    z = np.concatenate([tl, np.cos(angz), -np.sin(angz)], -1).astype(np.float32)
    zT = np.ascontiguousarray(z.T)
    tneg = np.ascontiguousarray((-tl[:, 0]).reshape(NT, 128).T).astype(np.float32)
    absd = np.abs(np.linspace(math.log(1e-2) / 1.5, math.log(1e-2) / 0.3, 512, dtype=np.float32))[None, :].astype(np.float32)
    k = np.arange(S)
    kt = (t[:, None].astype(np.int64) * k[None, :]) % n
    ph = kt * (2.0 * np.pi / n)
    sgn = np.where(t % 2 == 0, 1.0, -1.0)
    Fre = np.cos(ph)
    Fim = -np.sin(ph)
    Fim[:, 0] = sgn
    F = np.concatenate([Fre, Fim], 1)
    FWD = np.ascontiguousarray(F.reshape(NT, 128, 2 * NT, 128).transpose(2, 1, 0, 3)).astype(ml_dtypes.bfloat16)
    Gre = (2.0 / n) * Fre.T
    Gre[0, :] = 1.0 / n
    Gim = (2.0 / n) * (-np.sin(ph.T))
    Gim[0, :] = sgn / n
    Gm = np.concatenate([Gre, Gim], 0)
    INV = np.ascontiguousarray(Gm.reshape(2 * NT, 128, NT, 128).transpose(2, 1, 0, 3)).astype(ml_dtypes.bfloat16)
    return dict(c_rope=rope, c_zT=zT, c_tneg=tneg, c_absd=absd, c_fwd=FWD, c_inv=INV)


_W2D = {
    "emb_ln_g": (1, D), "emb_ln_b": (1, D), "w_in": (D, 1952), "q_norm_g": (1, 256), "w_uq": (256, 768),
    "kv_norm_g": (1, 128), "w_ukv": (128, 1024), "hy_short_w": (1, 3 * 1536), "hy_short_b": (1, 1536),
    "hy_filt_w1": (33, 64), "hy_filt_b1": (64, 1), "hy_filt_freq1": (64, 1), "hy_filt_w2": (64, 64),
    "hy_filt_b2": (64, 1), "hy_filt_freq2": (64, 1), "hy_filt_w3": (64, 2048), "hy_bias": (1, 1024),
    "attn_out_g": (1, 512), "hy_out_g": (1, 512), "w_o": (D, D), "ln_mix_g": (1, D), "ln_mix_b": (1, D),
    "peer_wq": (D, D), "peer_sub_keys": (8, 2, 128, 64), "peer_u": (16384, D), "peer_v": (16384, D),
    "ln_ffn_g": (1, D), "ln_ffn_b": (1, D),
}


def run(inputs, ncores, NB):
    x = np.asarray(inputs["x"], dtype=np.float32)
    B, S, _ = x.shape
    assert B == ncores * NB
    nc = build(S, NB)
    shared = _consts(S)
    for name, shp in _W2D.items():
        shared[name] = np.ascontiguousarray(np.asarray(inputs[name], dtype=np.float32).reshape(shp))
    in_maps = []
    for c in range(ncores):
        m = dict(shared)
        m["x"] = np.ascontiguousarray(x[c * NB:(c + 1) * NB])
        in_maps.append(m)
    res = run_bass_kernel_spmd(nc, in_maps, core_ids=list(range(ncores)))
    return np.concatenate([np.asarray(r["out"]) for r in res.results], axis=0).astype(np.float32)


def kernel(**inputs):
    return run(inputs, 8, 2)
```
